# Optimizing a Trainium2 kernel written in Bass

```python
import math
import jax, jax.numpy as jnp
from jax import lax
import numpy as np


D_MODEL = 1024
BATCH = 4
SEQ = 8192
DEPTH = 1

D_MIX = D_MODEL
HYENA_WIDTH = D_MIX // 2
POOL_WIDTH = D_MIX - HYENA_WIDTH
HYENA_ORDER = 2
N_DIRECTIONS = 2
SHORT_CONV = 3
FILTER_EMB = 33
FILTER_BANDS = (FILTER_EMB - 1) // 2
FILTER_HIDDEN = 64
FAST_DECAY_PCT = 0.3
SLOW_DECAY_PCT = 1.5
DECAY_TARGET = 1e-2
POOL_WINDOWS = (2, 4, 8, 16)
POOL_GROUP = POOL_WIDTH // len(POOL_WINDOWS)
HYENA_IN = (HYENA_ORDER + 1) * HYENA_WIDTH
PROJ_WIDTH = HYENA_IN + HYENA_WIDTH + 2 * POOL_WIDTH
EPS = 1e-6

kernel_name = 'hyena_pool_hybrid_block'


def rms_norm(x, g):
    xf = x.astype(jnp.float32)
    y = xf * lax.rsqrt(jnp.mean(xf * xf, axis=-1, keepdims=True) + EPS)
    return (y * g.astype(jnp.float32)).astype(x.dtype)


def short_conv(u, w, b):
    L = u.shape[1]
    up = jnp.pad(u, ((0, 0), (1, 1), (0, 0)))
    return up[:, :L] * w[0] + up[:, 1:L + 1] * w[1] + up[:, 2:] * w[2] + b


def hyena_filters(L, w1, b1, w2, b2, w3, b3, freq, w_proj):
    f32 = jnp.float32
    w1, b1, w2, b2, w3, b3, freq, w_proj = [a.astype(f32) for a in (w1, b1, w2, b2, w3, b3, freq, w_proj)]
    t = jnp.linspace(0.0, 1.0, L, dtype=f32)[:, None]
    pos = jnp.arange(L, dtype=f32)[:, None]
    ang = 2.0 * math.pi * pos / L
    bands = jnp.linspace(1e-4, FILTER_BANDS - 1, FILTER_BANDS, dtype=f32)[None, :]
    z = jnp.concatenate([t, jnp.cos(bands * ang), -jnp.sin(bands * ang)], axis=-1)
    h = jnp.sin(freq * (z @ w1 + b1))
    h = jnp.sin(freq * (h @ w2 + b2))
    h = jnp.sin(freq * (h @ w3 + b3))
    h = (h @ w_proj).reshape(L, HYENA_ORDER, N_DIRECTIONS, HYENA_WIDTH)
    max_decay = math.log(DECAY_TARGET) / FAST_DECAY_PCT
    min_decay = math.log(DECAY_TARGET) / SLOW_DECAY_PCT
    deltas = jnp.linspace(min_decay, max_decay, HYENA_WIDTH, dtype=f32)
    decay = jnp.exp(-t * jnp.abs(deltas)[None, :])
    return h * decay[:, None, None, :]


def two_sided_spectrum(h):
    L = h.shape[0]
    k_fwd = h[:, :, 0]
    k_bwd = h[:, :, 1]
    k_full = jnp.concatenate([k_fwd, jnp.zeros_like(k_fwd[:1]), k_bwd[:0:-1]], axis=0)
    return jnp.fft.rfft(k_full, n=2 * L, axis=0)


def fft_conv(u, k_f, d):
    L = u.shape[1]
    uf = u.astype(jnp.float32)
    u_f = jnp.fft.rfft(uf, n=2 * L, axis=1)
    y = jnp.fft.irfft(u_f * k_f[None], n=2 * L, axis=1)[:, :L]
    return (y + uf * d.astype(jnp.float32)).astype(u.dtype)


def multiscale_pool(u, pool_w, pool_scale):
    B, L, C = u.shape
    uf = u.astype(jnp.float32)
    S = jnp.concatenate([jnp.zeros((B, 1, C), jnp.float32), jnp.cumsum(uf, axis=1)], axis=1)
    pos = jnp.arange(L)
    outs = []
    for g, w in enumerate(POOL_WINDOWS):
        sl = slice(g * POOL_GROUP, (g + 1) * POOL_GROUP)
        lo = jnp.clip(pos - w // 2, 0, L - 1)
        hi = jnp.clip(pos + (w - 1 - w // 2), 0, L - 1)
        total = jnp.take(S[..., sl], hi + 1, axis=1) - jnp.take(S[..., sl], lo, axis=1)
        cnt = (hi - lo + 1).astype(jnp.float32)[None, :, None]
        pooled = total / cnt - uf[..., sl]
        outs.append(jnp.einsum('blc,cd->bld', pooled, pool_w[g].astype(jnp.float32)))
    y = jnp.concatenate(outs, axis=-1) * pool_scale.astype(jnp.float32)
    return y.astype(u.dtype)


def hybrid_layer(x, pre_g, w_in, conv_w, conv_b, fw1, fb1, fw2, fb2, fw3, fb3, ffreq, fw_out,
                 hyena_d, pool_w, pool_scale, norm_h_g, norm_p_g, w_out, post_g):
    L = x.shape[1]
    h = rms_norm(x, pre_g)
    p = jnp.einsum('bld,dk->blk', h, w_in)
    hy = short_conv(p[..., :HYENA_IN], conv_w, conv_b)
    v = hy[..., :HYENA_WIDTH]
    gates = (hy[..., HYENA_WIDTH:2 * HYENA_WIDTH], hy[..., 2 * HYENA_WIDTH:])
    z_h = p[..., HYENA_IN:HYENA_IN + HYENA_WIDTH]
    u_p = p[..., HYENA_IN + HYENA_WIDTH:HYENA_IN + HYENA_WIDTH + POOL_WIDTH]
    z_p = p[..., HYENA_IN + HYENA_WIDTH + POOL_WIDTH:]
    k_f = two_sided_spectrum(hyena_filters(L, fw1, fb1, fw2, fb2, fw3, fb3, ffreq, fw_out))
    y = v
    for o in range(HYENA_ORDER):
        y = gates[o] * fft_conv(y, k_f[:, o], hyena_d[o])
    y_h = rms_norm(y * jax.nn.silu(z_h), norm_h_g)
    y_p = rms_norm(multiscale_pool(u_p, pool_w, pool_scale) * jax.nn.silu(z_p), norm_p_g)
    out = jnp.einsum('blk,kd->bld', jnp.concatenate([y_h, y_p], axis=-1), w_out)
    return x + rms_norm(out, post_g)


def setup_inputs(seed: int = 0) -> dict:
    key = jax.random.key(seed)
    ks = jax.random.split(key, 24)

    def nrm(k, shape, scale):
        return jax.random.normal(k, shape, jnp.float32) * scale

    n_filt_out = HYENA_ORDER * N_DIRECTIONS * HYENA_WIDTH
    return {
        'x': nrm(ks[0], (BATCH, SEQ, D_MODEL), 1.0),
        'pre_norm_g': 1.0 + nrm(ks[1], (DEPTH, D_MODEL), 0.02),
        'w_in': nrm(ks[2], (DEPTH, D_MODEL, PROJ_WIDTH), D_MODEL ** -0.5),
        'conv_w': nrm(ks[3], (DEPTH, SHORT_CONV, HYENA_IN), SHORT_CONV ** -0.5),
        'conv_b': nrm(ks[4], (DEPTH, HYENA_IN), 0.01),
        'filt_w1': nrm(ks[5], (DEPTH, FILTER_EMB, FILTER_HIDDEN), FILTER_EMB ** -0.5),
        'filt_b1': nrm(ks[6], (DEPTH, FILTER_HIDDEN), 0.1),
        'filt_w2': nrm(ks[7], (DEPTH, FILTER_HIDDEN, FILTER_HIDDEN), FILTER_HIDDEN ** -0.5),
        'filt_b2': nrm(ks[8], (DEPTH, FILTER_HIDDEN), 0.1),
        'filt_w3': nrm(ks[9], (DEPTH, FILTER_HIDDEN, FILTER_HIDDEN), FILTER_HIDDEN ** -0.5),
        'filt_b3': nrm(ks[10], (DEPTH, FILTER_HIDDEN), 0.1),
        'filt_freq': 1.0 + nrm(ks[11], (DEPTH, FILTER_HIDDEN), 0.02),
        'filt_w_out': nrm(ks[12], (DEPTH, FILTER_HIDDEN, n_filt_out), 0.5 * FILTER_HIDDEN ** -0.5),
        'hyena_d': nrm(ks[13], (DEPTH, HYENA_ORDER, HYENA_WIDTH), 0.5),
        'pool_w': nrm(ks[14], (DEPTH, len(POOL_WINDOWS), POOL_GROUP, POOL_GROUP), POOL_GROUP ** -0.5),
        'pool_scale': 1.0 + nrm(ks[15], (DEPTH, POOL_WIDTH), 0.02),
        'norm_h_g': 1.0 + nrm(ks[16], (DEPTH, HYENA_WIDTH), 0.02),
        'norm_p_g': 1.0 + nrm(ks[17], (DEPTH, POOL_WIDTH), 0.02),
        'w_out': nrm(ks[18], (DEPTH, D_MIX, D_MODEL), D_MIX ** -0.5),
        'post_norm_g': 1.0 + nrm(ks[19], (DEPTH, D_MODEL), 0.02),
    }


def reference(x, pre_norm_g, w_in, conv_w, conv_b, filt_w1, filt_b1, filt_w2, filt_b2, filt_w3, filt_b3,
              filt_freq, filt_w_out, hyena_d, pool_w, pool_scale, norm_h_g, norm_p_g, w_out, post_norm_g):
    for i in range(DEPTH):
        x = hybrid_layer(x, pre_norm_g[i], w_in[i], conv_w[i], conv_b[i],
                         filt_w1[i], filt_b1[i], filt_w2[i], filt_b2[i], filt_w3[i], filt_b3[i],
                         filt_freq[i], filt_w_out[i], hyena_d[i], pool_w[i], pool_scale[i],
                         norm_h_g[i], norm_p_g[i], w_out[i], post_norm_g[i])
    return x
```

```python
import contextlib
import math
import numpy as np
import ml_dtypes
import concourse.bass as bass
import concourse.mybir as mybir
from concourse.bass_utils import run_bass_kernel_spmd

F32 = mybir.dt.float32
BF16 = mybir.dt.bfloat16
U8 = mybir.dt.uint8
ALU = mybir.AluOpType
AF = mybir.ActivationFunctionType
AP = bass.AP
NPBF = ml_dtypes.bfloat16

L = 8192
HL = 4096
DM = 1024
NFFT = 16384
EPS = 1e-6
HALO = 8
TW = 272
NT = 16
import os as _os
STRICT = bool(int(_os.environ.get('KSTRICT', '1')))
CG = 64
NG = 8


class Sched:
    CE = ('pe', 'act', 'dve', 'pool')
    NDMA = 8
    EPOCH = 3500

    def __init__(self, nc):
        self.nc = nc
        self.ops = {e: [] for e in ('pe', 'act', 'dve', 'pool', 'sp')}
        self.cnt = {e: 0 for e in self.CE}
        self.dma_uses = {}
        self.dma_rr = {'sp': 0, 'act': 0, 'pool': 0}
        self.lastw = {}
        self.readers = {}
        self.waited = {e: {} for e in self.ops}
        self.semnames = set()
        self.pending = {e: {} for e in self.ops}

    def barrier(self):
        toks = {e: c for e, c in self.cnt.items() if c > 0}
        for k, u in self.dma_uses.items():
            toks[k] = 16 * u
        for e in self.ops:
            for k, v in toks.items():
                if k != e and self.pending[e].get(k, 0) < v:
                    self.pending[e][k] = v

    def _deps(self, eng, reads, writes):
        toks = []
        for r in reads:
            t = self.lastw.get(r)
            if t is not None:
                toks.append((t, 'raw'))
        for w in writes:
            t = self.lastw.get(w)
            if t is not None:
                toks.append((t, 'waw'))
            for t in self.readers.get(w, ()):
                toks.append((t, 'war'))
        need = {}
        for (key, val), kind in toks:
            if key == eng and (eng == 'pe' or (not STRICT and eng in ('act', 'dve') and kind != 'raw')):
                continue
            if need.get(key, 0) < val:
                need[key] = val
        for k, v in self.pending[eng].items():
            if need.get(k, 0) < v:
                need[k] = v
        self.pending[eng] = {}
        out = {}
        for k, v in need.items():
            if self.waited[eng].get(k, 0) >= v:
                continue
            self.waited[eng][k] = v
            out[k] = v
        return out

    def _commit(self, tok, reads, writes):
        for r in reads:
            self.readers.setdefault(r, []).append(tok)
        for w in writes:
            self.lastw[w] = tok
            self.readers[w] = []

    def op(self, eng, fn, reads=(), writes=(), quiet=False):
        waits = self._deps(eng, reads, writes)
        self.semnames.add(eng)
        if quiet:
            tok = (eng, self.cnt[eng] + 1)
            self.ops[eng].append((sorted(waits.items(), key=str), fn, None))
        else:
            self.cnt[eng] += 1
            tok = (eng, self.cnt[eng])
            self.ops[eng].append((sorted(waits.items(), key=str), fn, (eng, 1)))
        self._commit(tok, reads, writes)

    def dma(self, fn, reads=(), writes=(), q='sp'):
        i = self.dma_rr[q]
        self.dma_rr[q] = (i + 1) % self.NDMA
        key = ('dma', q, i)
        self.semnames.add(key)
        uses = self.dma_uses.get(key, 0)
        waits = self._deps(q, reads, writes)
        if uses > 0 and self.waited[q].get(key, 0) < 16 * uses:
            waits[key] = 16 * uses
            self.waited[q][key] = 16 * uses
        self.dma_uses[key] = uses + 1
        tok = (key, 16 * (uses + 1))
        self.ops[q].append((sorted(waits.items(), key=str), fn, (key, 16)))
        self._commit(tok, reads, writes)

    def emit(self):
        nc = self.nc
        with contextlib.ExitStack() as st:
            sems = {}
            E = self.EPOCH
            for k in sorted(self.semnames, key=str):
                if isinstance(k, str):
                    for ep in range((self.cnt[k] + E - 1) // E):
                        sems[(k, ep)] = st.enter_context(nc.semaphore("s_%s_%d" % (k, ep)))
                else:
                    sems[k] = st.enter_context(nc.semaphore("s_d_%s_%d" % (k[1], k[2])))
            block = st.enter_context(nc.Block())
            finals = [(k, 16 * u) for k, u in self.dma_uses.items()]

            def semval(k, v):
                if isinstance(k, str):
                    ep = (v - 1) // E
                    return sems[(k, ep)], v - ep * E
                return sems[k], v

            def run(e, name):
                n = 0
                for waits, fn, sinc in self.ops[name]:
                    for k, v in waits:
                        sm, vv = semval(k, v)
                        e.wait_ge(sm, vv)
                    if sinc is None:
                        fn(e)
                        continue
                    sk, inc = sinc
                    if isinstance(sk, str):
                        n += 1
                        fn(e).then_inc(sems[(sk, (n - 1) // E)], inc)
                    else:
                        fn(e).then_inc(sems[sk], inc)

            @block.sync
            def _(e):
                run(e, 'sp')
                for k, v in finals:
                    e.wait_ge(sems[k], v)

            @block.tensor
            def _(e):
                run(e, 'pe')

            @block.scalar
            def _(e):
                run(e, 'act')

            @block.vector
            def _(e):
                run(e, 'dve')

            @block.gpsimd
            def _(e):
                run(e, 'pool')


_TAB_CACHE = {}


def _atrue(h):
    ap_ = np.arange(64)
    return np.where(ap_ < 32, 32 * h + ap_, 32 * (1 - h) + (ap_ - 32))


def shared_tables():
    if 'sh' in _TAB_CACHE:
        return _TAB_CACHE['sh']
    t = {}
    a = np.arange(128, dtype=np.float64)[:, None]
    c = (np.arange(64, dtype=np.float64) + 0.5)[None, :]
    ang = 2 * np.pi * a * c / 128.0
    F1r, F1i = np.cos(ang), -np.sin(ang)
    t['f1full'] = np.concatenate([F1r, F1i, -F1r], axis=1)
    b = np.arange(128, dtype=np.float64)[:, None, None]
    cc = (np.arange(64, dtype=np.float64) + 0.5)[None, :, None]
    d = np.arange(128, dtype=np.float64)[None, None, :]
    ph = 2 * np.pi * b * (cc + 128.0 * d) / NFFT
    t['gtab'] = np.stack([np.cos(ph), np.sin(ph)], axis=2).reshape(128, 64 * 2 * 128).astype(NPBF)
    dd = np.arange(128, dtype=np.float64)[:, None]
    bb = np.arange(128, dtype=np.float64)[None, :]
    ph2 = 2 * np.pi * dd * bb / 128.0
    t['ctab'] = np.concatenate([np.cos(ph2), -np.sin(ph2)], axis=1).astype(NPBF)
    aa = np.arange(128)[None, :]
    b2 = np.arange(128)[:, None]
    n = 128 * aa + b2
    pos = np.where(n < L, n, 2 * L - n)
    pos = np.where(n == L, 0, pos).astype(np.float64).reshape(-1)
    tt = pos / (L - 1)
    angp = 2.0 * np.pi * pos / L
    bands = np.linspace(1e-4, 15.0, 16)
    z = np.concatenate([tt[None, :], np.cos(bands[:, None] * angp[None, :]),
                        -np.sin(bands[:, None] * angp[None, :])], axis=0)
    t['ztab'] = z.astype(NPBF)
    max_decay = math.log(1e-2) / 0.3
    min_decay = math.log(1e-2) / 1.5
    deltas = np.abs(np.linspace(min_decay, max_decay, 512))
    n3 = 128 * np.arange(128)[:, None] + np.arange(128)[None, :]
    pos3 = np.where(n3 < L, n3, 2 * L - n3).astype(np.float64) / (L - 1)
    sgn = np.where(n3 < L, 1.0, -1.0)
    sgn = np.where(n3 == L, 0.0, sgn)
    dk = np.exp(-pos3[:, None, :] * deltas[None, :, None]) * sgn[:, None, :]
    t['dtab'] = dk.reshape(128, 512 * 128).astype(NPBF)
    t['ident'] = np.eye(128).astype(NPBF)
    _TAB_CACHE['sh'] = t
    return t


def core_tables(h):
    key = ('c', h)
    if key in _TAB_CACHE:
        return _TAB_CACHE[key]
    sh = shared_tables()
    at = _atrue(h)
    t = {}
    f1 = sh['f1full']
    t['f1tab'] = np.concatenate([f1, np.concatenate([f1[at], np.zeros((64, 192))], axis=0)], axis=1).astype(NPBF)
    m = (np.arange(64, dtype=np.float64) + 0.5)
    n = 128.0 * at[None, None, :] + np.arange(128, dtype=np.float64)[None, :, None]
    psi = 2 * np.pi * m[:, None, None] * n / NFFT
    H = np.concatenate([np.cos(psi), -np.sin(psi)], axis=0) * (2.0 / NFFT)
    t['htab'] = H.reshape(128, 128 * 64).astype(NPBF)
    tpos = 4096 * h + np.arange(HL)
    inv = []
    for w in (2, 4, 8, 16):
        lo = np.clip(tpos - w // 2, 0, L - 1)
        hi = np.clip(tpos + (w - 1 - w // 2), 0, L - 1)
        inv.append(1.0 / (hi - lo + 1))
    t['invc'] = np.stack(inv, 0).astype(np.float32)
    _TAB_CACHE[key] = t
    return t


def seg_ext(xb, start):
    out = np.zeros((HL + 2 * HALO, DM), np.float32)
    lo, hi = start - HALO, start + HL + HALO
    slo, shi = max(lo, 0), min(hi, L)
    out[slo - lo:shi - lo] = xb[slo:shi]
    return out


WEIGHT_SPECS = [
    ('pre_norm_g', [1, DM]), ('w_in', [DM, 3072]), ('conv_w', [3, 1536]), ('conv_b', [1, 1536]),
    ('filt_w1', [33, 64]), ('filt_b1', [1, 64]), ('filt_w2', [64, 64]), ('filt_b2', [1, 64]),
    ('filt_w3', [64, 64]), ('filt_b3', [1, 64]), ('filt_freq', [1, 64]), ('filt_w_out', [64, 2048]),
    ('hyena_d', [2, 512]), ('pool_w', [4, 128, 128]), ('pool_scale', [1, 512]),
    ('norm_h_g', [1, 512]), ('norm_p_g', [1, 512]), ('w_out', [DM, DM]), ('post_norm_g', [1, DM]),
]
TABLE_SPECS = [
    ('gtab', [128, 16384], BF16), ('ctab', [128, 256], BF16), ('ztab', [33, 16384], BF16),
    ('dtab', [128, 65536], BF16), ('ident', [128, 128], BF16), ('f1tab', [128, 384], BF16),
    ('htab', [128, 8192], BF16), ('invc', [4, HL], F32),
]


def build(phases=('F', 'A', 'B', 'C'), debug=False):
    nc = bass.Bass("TRN2", target_bir_lowering=False)
    S = Sched(nc)
    T = {}
    T['xm'] = nc.dram_tensor("xm", [HL + 2 * HALO, DM], F32, kind="ExternalInput")
    T['xo'] = nc.dram_tensor("xo", [HL + 2 * HALO, DM], F32, kind="ExternalInput")
    for nm, shp in WEIGHT_SPECS:
        T[nm] = nc.dram_tensor(nm, shp, F32, kind="ExternalInput")
    for nm, shp, dt in TABLE_SPECS:
        T[nm] = nc.dram_tensor(nm, shp, dt, kind="ExternalInput")
    out = nc.dram_tensor("out", [HL, DM], F32, kind="ExternalOutput")
    sk = "ExternalOutput" if debug else "Internal"
    v_s = nc.dram_tensor("v_s", [512, L], BF16, kind=sk)
    x1_s = nc.dram_tensor("x1_s", [512, L], BF16, kind=sk)
    x2_s = nc.dram_tensor("x2_s", [512, HL], BF16, kind=sk)
    szh_s = nc.dram_tensor("szh_s", [512, HL], BF16, kind=sk)
    szp_s = nc.dram_tensor("szp_s", [512, HL], BF16, kind=sk)
    ypr_s = nc.dram_tensor("ypr_s", [512, HL], BF16, kind=sk)
    c2_s = nc.dram_tensor("c2_s", [512, HL], BF16, kind=sk)
    y1_s = nc.dram_tensor("y1_s", [512, HL], BF16, kind=sk)
    k_s = nc.dram_tensor("k_s", [2 * NG, 128, 8192], BF16, kind=sk)

    ARENA = 204 * 1024
    arena = nc.alloc_sbuf_tensor("arena", [128, ARENA], U8)
    st = {'off': 0}

    def sb(shape, dt, parts=128):
        n = int(np.prod(shape)) * (4 if dt == F32 else 2)
        o = st['off']
        assert o + n <= ARENA, (o, n)
        v = arena[0:parts, o:o + n].bitcast(dt)
        st['off'] = (o + n + 63) // 64 * 64
        return v

    NPB = 6
    psb = [nc.alloc_psum_tensor("psb%d" % i, [128, 512], F32) for i in range(NPB)]
    pst = nc.alloc_psum_tensor("pst", [128, 1024], BF16)
    pst2 = nc.alloc_psum_tensor("pst2", [128, 1024], BF16)
    prr = {'i': 0}

    def psum():
        i = prr['i']
        prr['i'] = (i + 1) % NPB
        return psb[i], 'psb%d' % i

    def v3(ap, **kw):
        names = list(kw.keys())
        pat = "p (" + " ".join(names) + ") -> p " + " ".join(names)
        return ap.rearrange(pat, **kw)

    gtab = sb([16384], BF16)
    S.dma(lambda e: e.dma_start(out=gtab, in_=T['gtab'].ap()), writes=['gtab'])
    G4 = gtab.rearrange("p (c t d) -> p c t d", c=64, t=2)
    f1t = sb([384], BF16)
    S.dma(lambda e: e.dma_start(out=f1t, in_=T['f1tab'].ap()), writes=['f1t'])
    base_off = st['off']

    def fft_fwd(src_fn, src_reg, K, f1ap, nch, A_sb, U_sb, areg, ureg, after_bank=None):
        A4 = A_sb.rearrange("p (c q n) -> p c q n", c=64, q=3)
        A3 = A_sb.rearrange("p (c x) -> p c x", c=64)
        for ch0 in range(0, nch, 2):
            ps, pr = psum()
            for i in range(2):
                S.op('pe', lambda e, ps=ps, i=i, ch=ch0 + i: e.matmul(
                    ps[:, i * 192:(i + 1) * 192], lhsT=src_fn(ch), rhs=f1ap, start=True, stop=True),
                    reads=(src_reg if isinstance(src_reg, list) else [src_reg]) + ['f1t'], writes=[pr], quiet=(i == 0))
            eng = 'act' if (ch0 // 2) % 2 == 0 else 'dve'
            src = ps[:, 0:384].rearrange("p (i q c) -> p c q i", i=2, q=3)
            dst = A4[:, :, :, ch0:ch0 + 2]
            if eng == 'act':
                S.op('act', lambda e, s=src, d=dst: e.activation(out=d, in_=s, func=AF.Copy),
                     reads=[pr], writes=[areg])
            else:
                S.op('dve', lambda e, s=src, d=dst: e.tensor_copy(out=d, in_=s),
                     reads=[pr], writes=[areg])
        U3 = U_sb.rearrange("p (n c) -> p n c", c=64)
        cpb = 512 // (2 * nch)
        for c0 in range(0, 64, cpb):
            ps, pr = psum()
            for cc in range(cpb):
                c = c0 + cc
                S.op('pe', lambda e, ps=ps, c=c, cc=cc: e.matmul(
                    ps[:, cc * 2 * nch:(cc + 1) * 2 * nch], lhsT=G4[:, c, 0, :], rhs=A3[:, c, 0:2 * nch], start=True, stop=False),
                    reads=[areg, 'gtab'], writes=[pr], quiet=True)
                S.op('pe', lambda e, ps=ps, c=c, cc=cc: e.matmul(
                    ps[:, cc * 2 * nch:(cc + 1) * 2 * nch], lhsT=G4[:, c, 1, :], rhs=A3[:, c, nch:3 * nch], start=False, stop=True),
                    reads=[areg, 'gtab'], writes=[pr], quiet=(cc < cpb - 1))
            src = ps[:, 0:cpb * 2 * nch].rearrange("p (cc x) -> p x cc", cc=cpb)
            ur = ureg(c0) if callable(ureg) else ureg
            if (c0 // cpb) % 2 == 0 or after_bank is not None:
                S.op('act', lambda e, src=src, c0=c0: e.activation(out=U3[:, :, c0:c0 + cpb], in_=src, func=AF.Copy),
                     reads=[pr], writes=[ur])
            else:
                S.op('dve', lambda e, src=src, c0=c0: e.tensor_copy(out=U3[:, :, c0:c0 + cpb], in_=src),
                     reads=[pr], writes=[ur])
            if after_bank is not None:
                after_bank(c0 + cpb)

    if 'F' in phases:
        st['off'] = base_off
        h3s = sb([16384], BF16)
        wps = sb([1024], BF16)
        wpf = sb([1024], F32)
        zt = sb([512], BF16, parts=33)
        w1f = sb([64], F32, parts=33)
        w1b = sb([64], BF16, parts=33)
        w2f = sb([64], F32, parts=64)
        w2b = sb([64], BF16, parts=64)
        w3f = sb([64], F32, parts=64)
        w3b = sb([128], BF16, parts=64)
        prm = sb([8], F32)
        m2pi = sb([1], F32)
        u_t = sb([512], F32)
        r_t = sb([512], F32)
        h1t = sb([512], BF16, parts=64)
        h2t = sb([512], BF16, parts=64)
        kf = sb([2 * CG * 128], BF16)
        d_sb = sb([CG * 128], BF16)
        A_f = sb([64 * 3 * CG], BF16)
        K_f = sb([2 * CG * 64], BF16)

        S.dma(lambda e: e.dma_start(out=w1f, in_=T['filt_w1'].ap()), writes=['w1f'])
        S.dma(lambda e: e.dma_start(out=w2f, in_=T['filt_w2'].ap()), writes=['w2f'])
        S.dma(lambda e: e.dma_start(out=w3f, in_=T['filt_w3'].ap()), writes=['w3f'])
        for half in range(2):
            for j, nm in enumerate(('filt_freq', 'filt_b1', 'filt_b2', 'filt_b3')):
                S.dma(lambda e, half=half, j=j, nm=nm: e.dma_start(
                    out=prm[64 * half:64 * half + 64, j:j + 1], in_=AP(T[nm], 0, [[1, 64], [1, 1]])),
                    writes=['prm'])
            S.dma(lambda e, half=half: e.dma_start(
                out=wpf[64 * half:64 * half + 64, :].rearrange("p (o c) -> p o c", o=2),
                in_=AP(T['filt_w_out'], 512 * half, [[2048, 64], [1024, 2], [1, 512]])), writes=['wpf'])
        S.op('dve', lambda e: e.tensor_copy(out=wps, in_=wpf), reads=['wpf'], writes=['wps'])
        S.op('dve', lambda e: e.tensor_copy(out=w1b, in_=w1f), reads=['w1f'], writes=['w1b'])
        S.op('dve', lambda e: e.tensor_copy(out=w2b, in_=w2f), reads=['w2f'], writes=['w2b'])
        S.op('dve', lambda e: e.tensor_copy(out=w3b[:, 0:64], in_=w3f), reads=['w3f'], writes=['w3b'])
        S.op('dve', lambda e: e.tensor_copy(out=w3b[:, 64:128], in_=w3f), reads=['w3f'], writes=['w3b'])
        S.op('dve', lambda e: e.memset(m2pi, -2.0 * math.pi), writes=['m2pi'])
        S.op('dve', lambda e: e.tensor_scalar(out=prm[:, 4:5], in0=prm[:, 0:1], scalar1=1.0 / (2 * math.pi), scalar2=0.0,
                                              op0=ALU.mult, op1=ALU.add), reads=['prm'], writes=['prm'])
        for j in range(3):
            S.op('dve', lambda e, j=j: e.tensor_tensor(out=prm[:, 5 + j:6 + j], in0=prm[:, 1 + j:2 + j], in1=prm[:, 4:5],
                                                       op=ALU.mult), reads=['prm'], writes=['prm'])

        NQ = 4
        ZT = [zt] + [sb([512], BF16, parts=33) for _ in range(NQ - 1)]
        UT = [u_t] + [sb([512], F32) for _ in range(NQ - 1)]
        RT = [r_t] + [sb([512], F32) for _ in range(NQ - 1)]
        H1 = [h1t] + [sb([512], BF16, parts=64) for _ in range(NQ - 1)]
        H2 = [h2t] + [sb([512], BF16, parts=64) for _ in range(NQ - 1)]

        def sin_layers(lhs, lreg, np_, fbcol, rhs_fn, rreg_fn, out_fn, oreg_fn):
            pss = []
            for q in range(NQ):
                ps, pr = psum()
                pss.append((ps, pr))
                S.op('pe', lambda e, ps=ps, q=q: e.matmul(ps[0:np_, :], lhsT=lhs, rhs=rhs_fn(q), start=True, stop=True),
                     reads=[lreg, rreg_fn(q)], writes=[pr])
            for q in range(NQ):
                ps, pr = pss[q]
                S.op('act', lambda e, ps=ps, q=q: e.activation(out=UT[q][0:np_, :], in_=ps[0:np_, :], func=AF.Identity, scale=prm[0:np_, 4:5],
                                                              bias=prm[0:np_, fbcol:fbcol + 1]), reads=[pr, 'prm'], writes=['u_t%d' % q])
            for q in range(NQ):
                S.op('dve', lambda e, q=q: e.scalar_tensor_tensor(out=RT[q][0:np_, :], in0=UT[q][0:np_, :], scalar=-0.5, in1=UT[q][0:np_, :],
                                                                op0=ALU.is_lt, op1=ALU.add), reads=['u_t%d' % q], writes=['r_t%d' % q])
            for q in range(NQ):
                S.op('dve', lambda e, q=q: e.scalar_tensor_tensor(out=UT[q][0:np_, :], in0=RT[q][0:np_, :], scalar=0.5, in1=RT[q][0:np_, :],
                                                                op0=ALU.is_le, op1=ALU.add), reads=['r_t%d' % q], writes=['u_t%d' % q])
            for q in range(NQ):
                S.op('act', lambda e, q=q: e.activation(out=out_fn(q), in_=UT[q][0:np_, :], func=AF.Sin, scale=2 * math.pi,
                                                       bias=m2pi[0:np_, 0:1]), reads=['u_t%d' % q, 'm2pi'], writes=[oreg_fn(q)])

        for tj in range(0, 32, NQ):
            for q in range(NQ):
                ti = tj + q
                S.dma(lambda e, ti=ti, q=q: e.dma_start(out=ZT[q], in_=T['ztab'].ap()[:, ti * 512:(ti + 1) * 512]), writes=['zt%d' % q])
            sin_layers(w1b, 'w1b', 64, 5, lambda q: ZT[q], lambda q: 'zt%d' % q, lambda q: H1[q], lambda q: 'h1t%d' % q)
            sin_layers(w2b, 'w2b', 64, 6, lambda q: H1[q], lambda q: 'h1t%d' % q, lambda q: H2[q], lambda q: 'h2t%d' % q)
            sin_layers(w3b, 'w3b', 128, 7, lambda q: H2[q], lambda q: 'h2t%d' % q,
                       lambda q, tj=tj: h3s[:, (tj + q) * 512:(tj + q + 1) * 512], lambda q: 'h3s')
        h3v = h3s.rearrange("p (b a) -> p b a", a=128)
        S.op('pool', lambda e: e.memset(h3v[0:64, :, 64:128], 0.0), reads=['h3s'], writes=['h3s'])
        S.op('pool', lambda e: e.memset(h3v[64:128, :, 0:64], 0.0), reads=['h3s'], writes=['h3s'])

        wps3 = wps.rearrange("p (o c) -> p o c", o=2)
        kf4 = kf.rearrange("p (o c b) -> p o c b", o=2, c=CG)
        d3 = d_sb.rearrange("p (c b) -> p c b", c=CG)
        for g in range(int(_os.environ.get('FNG', NG))):
            S.dma(lambda e, g=g: e.dma_start(out=d_sb, in_=T['dtab'].ap()[:, g * CG * 128:(g + 1) * CG * 128]),
                  writes=['d_sb'])
            for bb in range(32):
                ps, pr = psum()
                for j in range(4):
                    b = 4 * bb + j
                    S.op('pe', lambda e, ps=ps, j=j, b=b, g=g: e.matmul(
                        ps[:, j * 128:(j + 1) * 128].rearrange("p (o c) -> p o c", o=2),
                        lhsT=h3s[:, b * 128:(b + 1) * 128], rhs=wps3[:, :, g * CG:(g + 1) * CG], start=True, stop=True),
                        reads=['h3s', 'wps'], writes=[pr], quiet=(j < 3))
                for o in range(2):
                    src = ps.rearrange("p (j o c) -> p o c j", j=4, o=2)[:, o]
                    S.op('dve', lambda e, src=src, o=o, bb=bb: e.tensor_tensor(
                        out=kf4[:, o, :, 4 * bb:4 * bb + 4], in0=src, in1=d3[:, :, 4 * bb:4 * bb + 4], op=ALU.mult),
                        reads=[pr, 'd_sb'], writes=['kf'])
            for o in range(2):
                fft_fwd(lambda ch, o=o: kf4[:, o, ch, :], 'kf', 128, f1t[:, 0:192], CG, A_f, K_f, 'A_f', 'K_f')
                S.dma(lambda e, o=o, g=g: e.dma_start(out=k_s.ap()[o * NG + g], in_=K_f), reads=['K_f'], writes=['k_s'])
        S.barrier()

    if 'A' in phases:
        st['off'] = base_off
        winb = sb([8 * 3072], BF16)
        wst = sb([3072], F32)
        gbc = sb([DM], F32)
        ident = sb([128], BF16)
        xt = [sb([3 * DM], F32) for _ in range(2)]
        xn = [sb([3 * DM], BF16) for _ in range(2)]
        junk = sb([DM], BF16)
        ssq = [sb([4], F32) for _ in range(2)]
        rstd = [sb([4], F32) for _ in range(2)]
        epsb = sb([1], F32)
        hT = [sb([8 * TW], BF16) for _ in range(2)]
        cwt = sb([4 * 12], F32)
        psc = sb([4], F32)
        pwf = sb([4 * 128], F32)
        pwb = sb([4 * 128], BF16)
        tA = [sb([256], F32) for _ in range(2)]
        tB = [sb([256], F32) for _ in range(2)]
        NOB = 4
        ob = [sb([256], BF16) for _ in range(NOB)]
        u_sbs = [sb([TW], F32) for _ in range(4)]
        s_as = [sb([TW], F32) for _ in range(4)]
        s_bs = [sb([TW], F32) for _ in range(4)]
        tPs = [sb([256], F32) for _ in range(4)]
        icbs = [[sb([256], F32) for _ in range(2)] for _ in range(4)]
        pooleds = [sb([256], BF16) for _ in range(4)]

        for kc in range(8):
            S.dma(lambda e, kc=kc: e.dma_start(out=wst, in_=T['w_in'].ap()[kc * 128:(kc + 1) * 128, :]), writes=['wst'])
            eng = 'dve' if kc % 2 == 0 else 'pool'
            S.op(eng, lambda e, kc=kc: e.tensor_copy(out=winb[:, kc * 3072:(kc + 1) * 3072], in_=wst),
                 reads=['wst'], writes=['winb'])
        S.dma(lambda e: e.dma_start(out=gbc, in_=AP(T['pre_norm_g'], 0, [[0, 128], [1, DM]])), writes=['gbc'])
        S.dma(lambda e: e.dma_start(out=ident, in_=T['ident'].ap()), writes=['ident'])
        cl_regs = []

        def colload(dst, col, src_t, off):
            cl_regs.append('colload%d' % len(cl_regs))
            S.dma(lambda e: e.dma_start(out=dst[:, col:col + 1], in_=AP(src_t, off, [[1, 128], [1, 1]])), writes=[cl_regs[-1]])
        for j in range(3):
            for m in range(12):
                colload(cwt, j * 12 + m, T['conv_w'], j * 1536 + m * 128)
        for m in range(12):
            colload(cwt, 36 + m, T['conv_b'], m * 128)
        for g in range(4):
            colload(psc, g, T['pool_scale'], g * 128)
        S.op('dve', lambda e: e.tensor_copy(out=cwt, in_=cwt), reads=list(cl_regs), writes=['cwt'])
        S.op('dve', lambda e: e.tensor_copy(out=psc, in_=psc), reads=list(cl_regs), writes=['psc'])
        S.dma(lambda e: e.dma_start(out=pwf.rearrange("p (g d) -> p g d", g=4),
                                    in_=AP(T['pool_w'], 0, [[128, 128], [128 * 128, 4], [1, 128]])), writes=['pwf'])
        S.op('dve', lambda e: e.tensor_copy(out=pwb, in_=pwf), reads=['pwf'], writes=['pwb'])
        S.op('dve', lambda e: e.memset(epsb, EPS), writes=['epsb'])
        winb3 = winb.rearrange("p (k n) -> p k n", k=8)
        hT3 = [h_.rearrange("p (k t) -> p k t", k=8) for h_ in hT]
        xt3 = [x_.rearrange("p (s d) -> p s d", s=3) for x_ in xt]
        xn3 = [x_.rearrange("p (s d) -> p s d", s=3) for x_ in xn]
        psts = [pst, pst2]
        obi = [0]
        tci = [0]

        def store(dst_t, row0, col0, src, sreg):
            S.dma(lambda e: e.dma_start(out=dst_t.ap()[row0:row0 + 128, col0:col0 + 256], in_=src),
                  reads=[sreg], writes=[dst_t.name + "_%d" % (row0 // 128)])

        NTT = 2 * NT

        def tinfo(i):
            seg, ti = divmod(i, NT)
            return seg, ti, (T['xm'] if seg == 0 else T['xo']), (list(range(24)) if seg == 0 else list(range(8)))

        def f_load(i):
            seg, ti, xsrc, _ = tinfo(i)
            p = i % 2
            r0 = 256 * ti
            S.dma(lambda e: e.dma_start(out=xt3[p][:, 0:2, :], in_=xsrc.ap()[r0:r0 + 256, :].rearrange("(s p) d -> p s d", p=128)),
                  writes=['xt%d' % p])
            S.dma(lambda e: e.dma_start(out=xt3[p][0:16, 2, :], in_=xsrc.ap()[r0 + 256:r0 + 272, :]), writes=['xt%d' % p])

        def f_norm(i):
            p = i % 2
            S.op('dve', lambda e: e.memset(ssq[p], 0.0), writes=['ssq%d' % p])
            for s_ in range(3):
                P_ = 128 if s_ < 2 else 16
                S.op('act', lambda e, s_=s_, P_=P_: e.activation(out=junk[0:P_, :], in_=xt3[p][0:P_, s_, :], func=AF.Square,
                                                                accum_out=ssq[p][0:P_, s_:s_ + 1]),
                     reads=['xt%d' % p], writes=['junk', 'ssq%d' % p])
            S.op('act', lambda e: e.activation(out=rstd[p][:, 0:3], in_=ssq[p][:, 0:3], func=AF.Sqrt, scale=1.0 / DM,
                                               bias=epsb[:, 0:1]), reads=['ssq%d' % p, 'epsb'], writes=['rstd%d' % p])
            S.op('dve', lambda e: e.reciprocal(out=rstd[p][:, 0:3], in_=rstd[p][:, 0:3]), reads=['rstd%d' % p], writes=['rstd%d' % p])
            for s_ in range(3):
                P_ = 128 if s_ < 2 else 16
                S.op('dve', lambda e, s_=s_, P_=P_: e.scalar_tensor_tensor(
                    out=xn3[p][0:P_, s_, :], in0=xt3[p][0:P_, s_, :], scalar=rstd[p][0:P_, s_:s_ + 1], in1=gbc[0:P_, :],
                    op0=ALU.mult, op1=ALU.mult), reads=['xt%d' % p, 'rstd%d' % p, 'gbc'], writes=['xn%d' % p])

        def f_T(i):
            p = i % 2
            for s_ in range(3):
                P_ = 128 if s_ < 2 else 16
                pt_ = psts[s_ % 2]
                preg = 'pst%d' % (s_ % 2)
                for kc in range(8):
                    S.op('pe', lambda e, s_=s_, kc=kc, P_=P_, pt_=pt_: e.transpose(
                        pt_[:, kc * 128:kc * 128 + P_], xn3[p][0:P_, s_, kc * 128:(kc + 1) * 128], ident[0:P_, 0:P_]),
                        reads=['xn%d' % p, 'ident'], writes=[preg], quiet=(kc < 7))
                S.op('dve', lambda e, s_=s_, P_=P_, pt_=pt_: e.tensor_copy(
                    out=hT3[p][:, :, s_ * 128:s_ * 128 + P_], in_=pt_[:, :].rearrange("p (k t) -> p k t", k=8)[:, :, 0:P_]),
                    reads=[preg], writes=['hT%d' % p])

        def back_chunk(i, m):
            seg, ti, _, _ = tinfo(i)
            p = i % 2
            n0 = HL * seg + 256 * ti
            ps, pr = psum()
            for kc in range(8):
                S.op('pe', lambda e, ps=ps, kc=kc: e.matmul(
                    ps[:, 0:TW], lhsT=winb3[:, kc, m * 128:(m + 1) * 128], rhs=hT3[p][:, kc, :],
                    start=(kc == 0), stop=(kc == 7)), reads=['winb', 'hT%d' % p], writes=[pr], quiet=(kc < 7))
            if m < 12:
                o_ = ob[obi[0] % NOB]
                oreg = 'ob%d' % (obi[0] % NOB)
                obi[0] += 1
                q_ = tci[0] % 2
                tci[0] += 1
                tA_, tB_ = tA[q_], tB[q_]
                S.op('act', lambda e: e.activation(
                    out=tA_, in_=ps[:, 8:264], func=AF.Identity, scale=cwt[:, 12 + m:13 + m], bias=cwt[:, 36 + m:37 + m]),
                    reads=[pr, 'cwt'], writes=['tA%d' % q_])
                S.op('dve', lambda e: e.scalar_tensor_tensor(
                    out=tB_, in0=ps[:, 7:263], scalar=cwt[:, m:m + 1], in1=tA_, op0=ALU.mult, op1=ALU.add),
                    reads=[pr, 'cwt', 'tA%d' % q_], writes=['tB%d' % q_])
                S.op('dve', lambda e: e.scalar_tensor_tensor(
                    out=o_, in0=ps[:, 9:265], scalar=cwt[:, 24 + m:25 + m], in1=tB_, op0=ALU.mult, op1=ALU.add),
                    reads=[pr, 'cwt', 'tB%d' % q_], writes=[oreg])
                if m < 4:
                    store(v_s, m * 128, n0, o_, oreg)
                elif m < 8:
                    store(x1_s, (m - 4) * 128, n0, o_, oreg)
                else:
                    store(x2_s, (m - 8) * 128, n0, o_, oreg)
            elif m < 16 or m >= 20:
                o_ = ob[obi[0] % NOB]
                oreg = 'ob%d' % (obi[0] % NOB)
                obi[0] += 1
                S.op('act', lambda e: e.activation(out=o_, in_=ps[:, 8:264], func=AF.Silu), reads=[pr], writes=[oreg])
                if m < 16:
                    store(szh_s, (m - 12) * 128, n0, o_, oreg)
                else:
                    store(szp_s, (m - 20) * 128, n0, o_, oreg)
            else:
                g = m - 16
                w = 2 << g
                u_sb, s_a, s_b, tP, icb, pooled = u_sbs[g], s_as[g], s_bs[g], tPs[g], icbs[g][i % 2], pooleds[g]
                S.op('act', lambda e: e.activation(out=u_sb, in_=ps[:, 0:TW], func=AF.Copy), reads=[pr], writes=['u_sb%d' % g])
                cur, creg, ln = u_sb, 'u_sb%d' % g, TW
                for k in range(g + 1):
                    sh = 1 << k
                    nxt, nreg = (s_a, 's_a%d' % g) if k % 2 == 0 else (s_b, 's_b%d' % g)
                    S.op('pool', lambda e, cur=cur, nxt=nxt, ln=ln, sh=sh: e.tensor_tensor(
                        out=nxt[:, 0:ln - sh], in0=cur[:, 0:ln - sh], in1=cur[:, sh:ln], op=ALU.add),
                        reads=[creg], writes=[nreg])
                    cur, creg, ln = nxt, nreg, ln - sh
                c0 = 8 - w // 2
                S.op('pool', lambda e, cur=cur: e.tensor_tensor(
                    out=tP, in0=cur[:, c0:c0 + 256], in1=icb, op=ALU.mult), reads=[creg, 'icb%d_%d' % (g, i % 2)], writes=['tP%d' % g])
                S.op('pool', lambda e: e.tensor_tensor(out=pooled, in0=tP, in1=u_sb[:, 8:264], op=ALU.subtract),
                     reads=['tP%d' % g, 'u_sb%d' % g], writes=['pooled%d' % g])

        def pool_part2(i, g):
            seg, ti, _, _ = tinfo(i)
            n0 = HL * seg + 256 * ti
            pooled = pooleds[g]
            ps2, pr2 = psum()
            S.op('pe', lambda e: e.matmul(ps2[:, 0:256], lhsT=pwb[:, g * 128:(g + 1) * 128], rhs=pooled,
                                          start=True, stop=True), reads=['pwb', 'pooled%d' % g], writes=[pr2])
            o_ = ob[obi[0] % NOB]
            oreg = 'ob%d' % (obi[0] % NOB)
            obi[0] += 1
            S.op('act', lambda e: e.activation(out=o_, in_=ps2[:, 0:256], func=AF.Copy, scale=psc[:, g:g + 1]),
                 reads=[pr2, 'psc'], writes=[oreg])
            store(ypr_s, g * 128, n0, o_, oreg)

        def icb_loads(i):
            seg, ti, _, _ = tinfo(i)
            n0 = 256 * ti
            for g in range(4):
                S.dma(lambda e, g=g: e.dma_start(out=icbs[g][i % 2], in_=AP(T['invc'], g * HL + n0, [[0, 128], [1, 256]])),
                      writes=['icb%d_%d' % (g, i % 2)])

        f_load(0)
        f_load(1)
        f_norm(0)
        f_T(0)
        for i in range(NTT):
            chunks = tinfo(i)[3]
            if len(chunks) > 8:
                zs = [12, 13, 14, 15, 20, 21, 22, 23]
                order = [16, 17, 18, 19]
                for q_ in range(6):
                    order += [2 * q_, 2 * q_ + 1, zs[q_]]
                order += zs[6:]
            else:
                order = list(chunks)
            tpos = (2 * len(order)) // 3
            if i == 0:
                icb_loads(0)
            if i + 1 < NT:
                icb_loads(i + 1)
            p2at = {9: 0, 12: 1, 15: 2, 18: 3} if len(chunks) > 8 else {}
            for idx, m in enumerate(order):
                if idx in p2at:
                    pool_part2(i, p2at[idx])
                if idx == 0 and i + 2 < NTT:
                    f_load(i + 2)
                if idx == 2 and i + 1 < NTT:
                    f_norm(i + 1)
                if idx == tpos and i + 1 < NTT:
                    f_T(i + 1)
                back_chunk(i, m)
            if len(chunks) > 8 and False:
                pass
        S.barrier()

    if 'B' in phases:
        st['off'] = base_off
        htab = sb([8192], BF16)
        ctab = sb([256], BF16)
        dbc = sb([1024], F32, parts=64)
        v_sb = sb([CG * 128], BF16, parts=64)
        x1_sb = sb([CG * 128], BF16, parts=64)
        K_sb = [sb([2 * CG * 64], BF16) for _ in range(2)]
        A_b = sb([64 * 3 * CG], BF16)
        AY = sb([64 * 3 * CG], BF16)
        UZ = sb([2 * CG * 64], BF16)
        T1 = sb([CG * 16], BF16)
        T2 = sb([CG * 16], BF16)
        T3 = sb([CG * 16], BF16)
        T4 = sb([CG * 16], BF16)
        hbs = [sb([512], F32, parts=64) for _ in range(3)]
        S.dma(lambda e: e.dma_start(out=htab, in_=T['htab'].ap()), writes=['htab'])
        S.dma(lambda e: e.dma_start(out=ctab, in_=T['ctab'].ap()), writes=['ctab'])
        S.dma(lambda e: e.dma_start(out=dbc, in_=AP(T['hyena_d'], 0, [[0, 64], [1, 1024]])), writes=['dbc'])
        H3 = htab.rearrange("p (b a) -> p b a", a=64)
        v3_ = v_sb.rearrange("p (c b) -> p c b", c=CG)
        x13 = x1_sb.rearrange("p (c b) -> p c b", c=CG)
        U4 = UZ.rearrange("p (q n c) -> p q n c", q=2, n=CG)
        Y4 = AY.rearrange("p (n q c) -> p n q c", n=CG, q=3)
        Z3 = UZ.rearrange("p (b n) -> p b n", n=CG)
        Y3 = AY.rearrange("p (n x) -> p n x", n=CG)
        T13 = T1.rearrange("p (n c) -> p n c", c=16)
        T23 = T2.rearrange("p (n c) -> p n c", c=16)
        T33 = T3.rearrange("p (n c) -> p n c", c=16)
        T43 = T4.rearrange("p (n c) -> p n c", c=16)
        UQ = ['Uq0', 'Uq1', 'Uq2', 'Uq3']
        VS = ['v_sb%d' % i_ for i_ in range(16)]
        YQ = ['Yq0', 'Yq1', 'Yq2', 'Yq3']

        def load_tc(dst3, src_t, g, reg):
            for q4 in range(4):
                S.dma(lambda e, q4=q4: e.dma_start(out=dst3[:, 16 * q4:16 * q4 + 16, :],
                                                   in_=AP(src_t, (g * CG + 16 * q4) * L, [[128, 64], [L, 16], [1, 128]])), writes=(reg if isinstance(reg, list) else [reg]))

        def spectral_q(o, cq):
            K4 = K_sb[o].rearrange("p (q n c) -> p q n c", q=2, n=CG)
            kr = 'K_sb%d' % o
            cs = slice(cq * 16, (cq + 1) * 16)
            ur, yr = UQ[cq], YQ[cq]
            S.op('dve', lambda e: e.tensor_tensor(out=T13, in0=U4[:, 0, :, cs], in1=K4[:, 0, :, cs], op=ALU.mult), reads=[ur, kr], writes=['T1'])
            S.op('dve', lambda e: e.tensor_tensor(out=T23, in0=U4[:, 1, :, cs], in1=K4[:, 1, :, cs], op=ALU.mult), reads=[ur, kr], writes=['T2'])
            S.op('dve', lambda e: e.tensor_tensor(out=T33, in0=U4[:, 0, :, cs], in1=K4[:, 1, :, cs], op=ALU.mult), reads=[ur, kr], writes=['T3'])
            S.op('dve', lambda e: e.tensor_tensor(out=T43, in0=U4[:, 1, :, cs], in1=K4[:, 0, :, cs], op=ALU.mult), reads=[ur, kr], writes=['T4'])
            S.op('dve', lambda e: e.tensor_tensor(out=Y4[:, :, 0, cs], in0=T13, in1=T23, op=ALU.subtract), reads=['T1', 'T2'], writes=[yr])
            S.op('dve', lambda e: e.tensor_tensor(out=Y4[:, :, 2, cs], in0=T23, in1=T13, op=ALU.subtract), reads=['T1', 'T2'], writes=[yr])
            S.op('dve', lambda e: e.tensor_tensor(out=Y4[:, :, 1, cs], in0=T33, in1=T43, op=ALU.add), reads=['T3', 'T4'], writes=[yr])

        def conv_fwd(o):
            def ab(cend):
                if cend % 16 == 0:
                    spectral_q(o, cend // 16 - 1)
            fft_fwd(lambda ch: v3_[:, ch, :], VS, 64, f1t[0:64, 192:384], CG, A_b, UZ, 'A_b',
                    lambda c0: UQ[c0 // 16], after_bank=ab)

        def fft_inv(handler):
            for ch0 in range(0, CG, 4):
                ps, pr = psum()
                for i in range(4):
                    ch = ch0 + i
                    S.op('pe', lambda e, ps=ps, i=i, ch=ch: e.matmul(
                        ps[:, i * 128:(i + 1) * 128], lhsT=Y3[:, ch, 0:128], rhs=ctab[:, 0:128], start=True, stop=False),
                        reads=YQ + ['ctab'], writes=[pr], quiet=True)
                    S.op('pe', lambda e, ps=ps, i=i, ch=ch: e.matmul(
                        ps[:, i * 128:(i + 1) * 128], lhsT=Y3[:, ch, 64:192], rhs=ctab[:, 128:256], start=False, stop=True),
                        reads=YQ + ['ctab'], writes=[pr], quiet=(i < 3))
                src = ps.rearrange("p (i b) -> p b i", i=4)
                if (ch0 // 4) % 2 == 0:
                    S.op('act', lambda e, src=src, ch0=ch0: e.activation(out=Z3[:, :, ch0:ch0 + 4], in_=src, func=AF.Copy),
                         reads=[pr], writes=UQ)
                else:
                    S.op('dve', lambda e, src=src, ch0=ch0: e.tensor_copy(out=Z3[:, :, ch0:ch0 + 4], in_=src),
                         reads=[pr], writes=UQ)
            for bg in range(16):
                ps, pr = psum()
                for j in range(8):
                    b = 8 * bg + j
                    S.op('pe', lambda e, ps=ps, j=j, b=b: e.matmul(
                        ps[0:64, j * CG:(j + 1) * CG], lhsT=H3[:, b, :], rhs=Z3[:, b, :], start=True, stop=True),
                        reads=['htab'] + UQ, writes=[pr], quiet=(j < 7))
                handler(bg, ps[0:64, :].rearrange("p (j n) -> p n j", j=8), pr)

        import os
        BSTOP = int(os.environ.get('BSTOP', '99'))
        for g in range(int(os.environ.get('BNG', NG))):
            if g == 0:
                load_tc(v3_, v_s, g, VS)
            load_tc(x13, x1_s, g, 'x1_sb')
            for o in range(2):
                S.dma(lambda e, o=o, g=g: e.dma_start(out=K_sb[o], in_=k_s.ap()[o * NG + g]), writes=['K_sb%d' % o])
            if BSTOP < 1:
                continue
            conv_fwd(0)
            S.op('dve', lambda e, g=g: e.tensor_tensor(
                out=v3_, in0=v3_, in1=dbc[:, g * CG:(g + 1) * CG].unsqueeze(2).broadcast_to([64, CG, 128]), op=ALU.mult),
                reads=VS + ['dbc'], writes=VS)
            def h1(bg, psv, pr):
                hb3 = hbs[bg % 3].rearrange("p (n j) -> p n j", j=8)
                hr = 'hb%d' % (bg % 3)
                S.op('dve', lambda e: e.tensor_tensor(out=hb3, in0=psv, in1=v3_[:, :, 8 * bg:8 * bg + 8], op=ALU.add),
                     reads=[pr, 'v_sb%d' % bg], writes=[hr])
                S.op('pool', lambda e: e.tensor_tensor(out=v3_[:, :, 8 * bg:8 * bg + 8], in0=hb3, in1=x13[:, :, 8 * bg:8 * bg + 8],
                                                       op=ALU.mult), reads=[hr, 'x1_sb'], writes=['v_sb%d' % bg])
            if BSTOP < 4:
                continue
            fft_inv(h1)
            if BSTOP < 5:
                continue
            S.dma(lambda e, g=g: e.dma_start(out=AP(y1_s, g * CG * HL, [[128, 32], [HL, CG], [1, 128]]), in_=v3_[0:32]),
                  reads=VS, writes=['y1_s'])
            conv_fwd(1)
            if g + 1 < NG:
                load_tc(v3_, v_s, g + 1, VS)

            def h2(bg, psv, pr):
                S.op('act', lambda e: e.activation(out=x13[:, :, 8 * bg:8 * bg + 8], in_=psv, func=AF.Copy),
                     reads=[pr], writes=['x1_sb'])
            fft_inv(h2)
            S.dma(lambda e, g=g: e.dma_start(out=AP(c2_s, g * CG * HL, [[128, 32], [HL, CG], [1, 128]]), in_=x13[0:32]),
                  reads=['x1_sb'], writes=['c2_s'])
        S.barrier()

    if 'C' in phases:
        st['off'] = base_off
        woutb = sb([8 * DM], BF16)
        wst2 = sb([DM], F32)
        pgbc = sb([DM], F32)
        cst = sb([12], F32)
        ones = sb([1], BF16)
        epsc = sb([1], F32)
        LDN = ('c2', 'y1', 'x2', 'szh', 'ypr', 'szp')
        ld = [{nm: sb([4 * 256], BF16) for nm in LDN} for _ in range(2)]
        xr = [sb([2 * DM], F32) for _ in range(2)]
        t1s = [sb([256], F32) for _ in range(4)]
        t2s = [sb([256], F32) for _ in range(4)]
        t3s = [sb([256], F32) for _ in range(4)]
        yvs = [sb([256], F32) for _ in range(4)]
        sq = [{G: sb([4 * 256], BF16) for G in 'hp'} for _ in range(2)]
        yg = [{G: sb([4 * 256], BF16) for G in 'hp'} for _ in range(2)]
        rsg = [sb([4], F32) for _ in range(2)]
        o_sbs = [sb([DM], F32) for _ in range(4)]
        junk2 = sb([DM], BF16)
        sso = [sb([2], F32) for _ in range(2)]
        for kc in range(8):
            S.dma(lambda e, kc=kc: e.dma_start(out=wst2, in_=T['w_out'].ap()[kc * 128:(kc + 1) * 128, :]), writes=['wst2'])
            S.op('dve', lambda e, kc=kc: e.tensor_copy(out=woutb[:, kc * DM:(kc + 1) * DM], in_=wst2), reads=['wst2'], writes=['woutb'])
        S.dma(lambda e: e.dma_start(out=pgbc, in_=AP(T['post_norm_g'], 0, [[0, 128], [1, DM]])), writes=['pgbc'])
        for k in range(4):
            S.dma(lambda e, k=k: e.dma_start(out=cst[:, k:k + 1], in_=AP(T['hyena_d'], 512 + 128 * k, [[1, 128], [1, 1]])), writes=['cstl%d' % (3 * k)])
            S.dma(lambda e, k=k: e.dma_start(out=cst[:, 4 + k:5 + k], in_=AP(T['norm_h_g'], 128 * k, [[1, 128], [1, 1]])), writes=['cstl%d' % (3 * k + 1)])
            S.dma(lambda e, k=k: e.dma_start(out=cst[:, 8 + k:9 + k], in_=AP(T['norm_p_g'], 128 * k, [[1, 128], [1, 1]])), writes=['cstl%d' % (3 * k + 2)])
        S.op('dve', lambda e: e.tensor_copy(out=cst, in_=cst), reads=['cstl%d' % q for q in range(12)], writes=['cst'])
        S.op('dve', lambda e: e.memset(ones, 1.0), writes=['ones'])
        S.op('dve', lambda e: e.memset(epsc, EPS), writes=['epsc'])
        wo3 = woutb.rearrange("p (k n) -> p k n", k=8)
        srcs = {'c2': c2_s, 'y1': y1_s, 'x2': x2_s, 'szh': szh_s, 'ypr': ypr_s, 'szp': szp_s}
        ld3 = [{k: v.rearrange("p (k t) -> p k t", k=4) for k, v in l_.items()} for l_ in ld]
        sq3 = [{k: v.rearrange("p (k t) -> p k t", k=4) for k, v in l_.items()} for l_ in sq]
        yg3 = [{k: v.rearrange("p (k t) -> p k t", k=4) for k, v in l_.items()} for l_ in yg]
        xr3 = [x_.rearrange("p (s d) -> p s d", s=2) for x_ in xr]
        import os
        NTC = int(os.environ.get('CNT', NT))

        def c_load(ti):
            p = ti % 2
            n0 = 256 * ti
            for nm in LDN:
                S.dma(lambda e, nm=nm: e.dma_start(out=ld3[p][nm], in_=AP(srcs[nm], n0, [[HL, 128], [128 * HL, 4], [1, 256]])),
                      writes=['ld%d_%s' % (p, nm)])

        def c_load_x(ti):
            p = ti % 2
            n0 = 256 * ti
            S.dma(lambda e: e.dma_start(out=xr3[p], in_=T['xm'].ap()[HALO + n0:HALO + n0 + 256, :].rearrange("(s p) d -> p s d", p=128)),
                  writes=['xr%d' % p])

        def c_front(ti):
            p = ti % 2
            L_ = ld3[p]
            lr = lambda nm: 'ld%d_%s' % (p, nm)
            for k in range(4):
                t1, t2, t3, yv = t1s[k], t2s[k], t3s[k], yvs[k]
                S.op('dve', lambda e, k=k, t1=t1: e.scalar_tensor_tensor(out=t1, in0=L_['y1'][:, k, :], scalar=cst[:, k:k + 1],
                                                                 in1=L_['c2'][:, k, :], op0=ALU.mult, op1=ALU.add),
                     reads=[lr('y1'), lr('c2'), 'cst'], writes=['t1_%d' % k])
                S.op('dve', lambda e, k=k, t1=t1, t2=t2: e.tensor_tensor(out=t2, in0=t1, in1=L_['x2'][:, k, :], op=ALU.mult),
                     reads=['t1_%d' % k, lr('x2')], writes=['t2_%d' % k])
                S.op('dve', lambda e, k=k, t2=t2, yv=yv: e.tensor_tensor(out=yv, in0=t2, in1=L_['szh'][:, k, :], op=ALU.mult),
                     reads=['t2_%d' % k, lr('szh')], writes=['yv_%d' % k])
                S.op('pool', lambda e, k=k, t3=t3: e.tensor_tensor(out=t3, in0=L_['ypr'][:, k, :], in1=L_['szp'][:, k, :], op=ALU.mult),
                     reads=[lr('ypr'), lr('szp')], writes=['t3_%d' % k])
            for k in range(4):
                t3, yv = t3s[k], yvs[k]
                S.op('act', lambda e, k=k, yv=yv: e.activation(out=sq3[p]['h'][:, k, :], in_=yv, func=AF.Square), reads=['yv_%d' % k], writes=['sq%d_h' % p])
                S.op('act', lambda e, k=k, yv=yv: e.activation(out=yg3[p]['h'][:, k, :], in_=yv, func=AF.Copy, scale=cst[:, 4 + k:5 + k]),
                     reads=['yv_%d' % k, 'cst'], writes=['yg%d_h' % p])
                S.op('act', lambda e, k=k, t3=t3: e.activation(out=sq3[p]['p'][:, k, :], in_=t3, func=AF.Square), reads=['t3_%d' % k], writes=['sq%d_p' % p])
                S.op('act', lambda e, k=k, t3=t3: e.activation(out=yg3[p]['p'][:, k, :], in_=t3, func=AF.Copy, scale=cst[:, 8 + k:9 + k]),
                     reads=['t3_%d' % k, 'cst'], writes=['yg%d_p' % p])

        oi = [0]

        def c_back(ti):
            p = ti % 2
            n0 = 256 * ti
            pss, prs = psum()
            for s_ in range(2):
                for gi, G in enumerate('hp'):
                    col = 2 * s_ + gi
                    for k in range(4):
                        S.op('pe', lambda e, s_=s_, G=G, k=k, col=col: e.matmul(
                            pss[:, col:col + 1], lhsT=sq3[p][G][:, k, s_ * 128:(s_ + 1) * 128], rhs=ones, start=(k == 0), stop=(k == 3)),
                            reads=['sq%d_%s' % (p, G), 'ones'], writes=[prs], quiet=not (s_ == 1 and gi == 1 and k == 3))
            S.op('act', lambda e: e.activation(out=rsg[p], in_=pss[:, 0:4], func=AF.Sqrt, scale=1.0 / 512, bias=epsc[:, 0:1]),
                 reads=[prs, 'epsc'], writes=['rsg%d' % p])
            S.op('dve', lambda e: e.reciprocal(out=rsg[p], in_=rsg[p]), reads=['rsg%d' % p], writes=['rsg%d' % p])
            for s_ in range(2):
                osb = o_sbs[oi[0] % 4]
                oreg = 'o_sb%d' % (oi[0] % 4)
                oi[0] += 1
                for n2 in range(2):
                    pp = {}
                    for gi, G in enumerate('hp'):
                        ps, pr = psum()
                        pp[G] = (ps, pr)
                        for k in range(4):
                            S.op('pe', lambda e, ps=ps, s_=s_, G=G, k=k, gi=gi, n2=n2: e.matmul(
                                ps[:, :], lhsT=yg3[p][G][:, k, s_ * 128:(s_ + 1) * 128], rhs=wo3[:, 4 * gi + k, n2 * 512:(n2 + 1) * 512],
                                start=(k == 0), stop=(k == 3)), reads=['yg%d_%s' % (p, G), 'woutb'], writes=[pr], quiet=(k < 3))
                    S.op('act', lambda e, pp=pp, s_=s_, n2=n2, osb=osb: e.activation(
                        out=osb[:, n2 * 512:(n2 + 1) * 512], in_=pp['h'][0][:, :], func=AF.Copy, scale=rsg[p][:, 2 * s_:2 * s_ + 1]),
                        reads=[pp['h'][1], 'rsg%d' % p], writes=[oreg])
                    S.op('dve', lambda e, pp=pp, s_=s_, n2=n2, osb=osb: e.scalar_tensor_tensor(
                        out=osb[:, n2 * 512:(n2 + 1) * 512], in0=pp['p'][0][:, :], scalar=rsg[p][:, 2 * s_ + 1:2 * s_ + 2],
                        in1=osb[:, n2 * 512:(n2 + 1) * 512], op0=ALU.mult, op1=ALU.add), reads=[pp['p'][1], 'rsg%d' % p, oreg], writes=[oreg])
                S.op('act', lambda e, osb=osb, s_=s_: e.activation(out=junk2, in_=osb, func=AF.Square, accum_out=sso[p][:, s_:s_ + 1]),
                     reads=[oreg], writes=['junk2', 'sso%d' % p])
                S.op('act', lambda e, s_=s_: e.activation(out=sso[p][:, s_:s_ + 1], in_=sso[p][:, s_:s_ + 1], func=AF.Sqrt, scale=1.0 / DM,
                                                         bias=epsc[:, 0:1]), reads=['sso%d' % p, 'epsc'], writes=['sso%d' % p])
                S.op('dve', lambda e, s_=s_: e.reciprocal(out=sso[p][:, s_:s_ + 1], in_=sso[p][:, s_:s_ + 1]), reads=['sso%d' % p], writes=['sso%d' % p])
                S.op('dve', lambda e, osb=osb, s_=s_: e.scalar_tensor_tensor(out=osb, in0=osb, scalar=sso[p][:, s_:s_ + 1], in1=pgbc,
                                                                            op0=ALU.mult, op1=ALU.mult),
                     reads=[oreg, 'sso%d' % p, 'pgbc'], writes=[oreg])
                S.op('pool', lambda e, osb=osb, s_=s_: e.tensor_tensor(out=osb, in0=osb, in1=xr3[p][:, s_, :], op=ALU.add),
                     reads=[oreg, 'xr%d' % p], writes=[oreg])
                S.dma(lambda e, osb=osb, s_=s_: e.dma_start(out=out.ap()[n0 + 128 * s_:n0 + 128 * s_ + 128, :], in_=osb),
                      reads=[oreg], writes=['out%d' % (oi[0] % 4)])

        c_load(0)
        c_load_x(0)
        if NTC > 1:
            c_load(1)
            c_load_x(1)
        c_front(0)
        for ti in range(NTC):
            if ti + 1 < NTC:
                c_front(ti + 1)
            if ti + 2 < NTC:
                c_load(ti + 2)
            c_back(ti)
            if ti + 2 < NTC:
                c_load_x(ti + 2)
    S.emit()
    return nc


_NC_CACHE = {}


def make_in_maps(inputs):
    x = np.asarray(inputs['x'], np.float32)
    sh = shared_tables()
    w = {}
    for nm, shp in WEIGHT_SPECS:
        w[nm] = np.ascontiguousarray(np.asarray(inputs[nm], np.float32).reshape(shp))
    maps = []
    for core in range(8):
        b, h = core // 2, core % 2
        ct = core_tables(h)
        m = dict(w)
        m['xm'] = seg_ext(x[b], HL * h)
        m['xo'] = seg_ext(x[b], HL * (1 - h))
        for nm in ('gtab', 'ctab', 'ztab', 'dtab', 'ident'):
            m[nm] = sh[nm]
        for nm in ('f1tab', 'htab', 'invc'):
            m[nm] = ct[nm]
        maps.append(m)
    return maps


def kernel(**inputs):
    if 'nc' not in _NC_CACHE:
        _NC_CACHE['nc'] = build()
    nc = _NC_CACHE['nc']
    maps = make_in_maps(inputs)
    res = run_bass_kernel_spmd(nc, maps, core_ids=list(range(8)))
    outp = np.zeros((4, L, DM), np.float32)
    for core in range(8):
        b, h = core // 2, core % 2
        outp[b, HL * h:HL * (h + 1)] = np.asarray(res.results[core]['out'], np.float32)
    return outp
```

```python
import contextlib
import math
import numpy as np
import ml_dtypes
import concourse.bass as bass
import concourse.mybir as mybir
from concourse.bass_utils import run_bass_kernel_spmd

F32 = mybir.dt.float32
BF16 = mybir.dt.bfloat16
U8 = mybir.dt.uint8
ALU = mybir.AluOpType
AF = mybir.ActivationFunctionType
AP = bass.AP
NPBF = ml_dtypes.bfloat16

L = 8192
HL = 4096
DM = 1024
NFFT = 16384
EPS = 1e-6
HALO = 8
TW = 272
NT = 16
import os as _os
STRICT = bool(int(_os.environ.get('KSTRICT', '1')))
CG = 64
NG = 8


class Sched:
    CE = ('pe', 'act', 'dve', 'pool')
    NDMA = 8
    EPOCH = 3500

    def __init__(self, nc):
        self.nc = nc
        self.ops = {e: [] for e in ('pe', 'act', 'dve', 'pool', 'sp')}
        self.cnt = {e: 0 for e in self.CE}
        self.dma_uses = {}
        self.dma_rr = {'sp': 0, 'act': 0, 'pool': 0}
        self.lastw = {}
        self.readers = {}
        self.waited = {e: {} for e in self.ops}
        self.semnames = set()
        self.pending = {e: {} for e in self.ops}

    def barrier(self):
        toks = {e: c for e, c in self.cnt.items() if c > 0}
        for k, u in self.dma_uses.items():
            toks[k] = 16 * u
        for e in self.ops:
            for k, v in toks.items():
                if k != e and self.pending[e].get(k, 0) < v:
                    self.pending[e][k] = v

    def _deps(self, eng, reads, writes):
        toks = []
        for r in reads:
            t = self.lastw.get(r)
            if t is not None:
                toks.append((t, 'raw'))
        for w in writes:
            t = self.lastw.get(w)
            if t is not None:
                toks.append((t, 'waw'))
            for t in self.readers.get(w, ()):
                toks.append((t, 'war'))
        need = {}
        for (key, val), kind in toks:
            if key == eng and (eng == 'pe' or (not STRICT and eng in ('act', 'dve') and kind != 'raw')):
                continue
            if need.get(key, 0) < val:
                need[key] = val
        for k, v in self.pending[eng].items():
            if need.get(k, 0) < v:
                need[k] = v
        self.pending[eng] = {}
        out = {}
        for k, v in need.items():
            if self.waited[eng].get(k, 0) >= v:
                continue
            self.waited[eng][k] = v
            out[k] = v
        return out

    def _commit(self, tok, reads, writes):
        for r in reads:
            self.readers.setdefault(r, []).append(tok)
        for w in writes:
            self.lastw[w] = tok
            self.readers[w] = []

    def op(self, eng, fn, reads=(), writes=(), quiet=False):
        waits = self._deps(eng, reads, writes)
        self.semnames.add(eng)
        if quiet:
            tok = (eng, self.cnt[eng] + 1)
            self.ops[eng].append((sorted(waits.items(), key=str), fn, None))
        else:
            self.cnt[eng] += 1
            tok = (eng, self.cnt[eng])
            self.ops[eng].append((sorted(waits.items(), key=str), fn, (eng, 1)))
        self._commit(tok, reads, writes)

    def dma(self, fn, reads=(), writes=(), q='sp'):
        i = self.dma_rr[q]
        self.dma_rr[q] = (i + 1) % self.NDMA
        key = ('dma', q, i)
        self.semnames.add(key)
        uses = self.dma_uses.get(key, 0)
        waits = self._deps(q, reads, writes)
        if uses > 0 and self.waited[q].get(key, 0) < 16 * uses:
            waits[key] = 16 * uses
            self.waited[q][key] = 16 * uses
        self.dma_uses[key] = uses + 1
        tok = (key, 16 * (uses + 1))
        self.ops[q].append((sorted(waits.items(), key=str), fn, (key, 16)))
        self._commit(tok, reads, writes)

    def emit(self):
        nc = self.nc
        with contextlib.ExitStack() as st:
            sems = {}
            E = self.EPOCH
            for k in sorted(self.semnames, key=str):
                if isinstance(k, str):
                    for ep in range((self.cnt[k] + E - 1) // E):
                        sems[(k, ep)] = st.enter_context(nc.semaphore("s_%s_%d" % (k, ep)))
                else:
                    sems[k] = st.enter_context(nc.semaphore("s_d_%s_%d" % (k[1], k[2])))
            block = st.enter_context(nc.Block())
            finals = [(k, 16 * u) for k, u in self.dma_uses.items()]

            def semval(k, v):
                if isinstance(k, str):
                    ep = (v - 1) // E
                    return sems[(k, ep)], v - ep * E
                return sems[k], v

            def run(e, name):
                n = 0
                for waits, fn, sinc in self.ops[name]:
                    for k, v in waits:
                        sm, vv = semval(k, v)
                        e.wait_ge(sm, vv)
                    if sinc is None:
                        fn(e)
                        continue
                    sk, inc = sinc
                    if isinstance(sk, str):
                        n += 1
                        fn(e).then_inc(sems[(sk, (n - 1) // E)], inc)
                    else:
                        fn(e).then_inc(sems[sk], inc)

            @block.sync
            def _(e):
                run(e, 'sp')
                for k, v in finals:
                    e.wait_ge(sems[k], v)

            @block.tensor
            def _(e):
                run(e, 'pe')

            @block.scalar
            def _(e):
                run(e, 'act')

            @block.vector
            def _(e):
                run(e, 'dve')

            @block.gpsimd
            def _(e):
                run(e, 'pool')


_TAB_CACHE = {}


def _atrue(h):
    ap_ = np.arange(64)
    return np.where(ap_ < 32, 32 * h + ap_, 32 * (1 - h) + (ap_ - 32))


def shared_tables():
    if 'sh' in _TAB_CACHE:
        return _TAB_CACHE['sh']
    t = {}
    a = np.arange(128, dtype=np.float64)[:, None]
    c = (np.arange(64, dtype=np.float64) + 0.5)[None, :]
    ang = 2 * np.pi * a * c / 128.0
    F1r, F1i = np.cos(ang), -np.sin(ang)
    t['f1full'] = np.concatenate([F1r, F1i, -F1r], axis=1)
    b = np.arange(128, dtype=np.float64)[:, None, None]
    cc = (np.arange(64, dtype=np.float64) + 0.5)[None, :, None]
    d = np.arange(128, dtype=np.float64)[None, None, :]
    ph = 2 * np.pi * b * (cc + 128.0 * d) / NFFT
    t['gtab'] = np.stack([np.cos(ph), np.sin(ph)], axis=2).reshape(128, 64 * 2 * 128).astype(NPBF)
    dd = np.arange(128, dtype=np.float64)[:, None]
    bb = np.arange(128, dtype=np.float64)[None, :]
    ph2 = 2 * np.pi * dd * bb / 128.0
    t['ctab'] = np.concatenate([np.cos(ph2), -np.sin(ph2)], axis=1).astype(NPBF)
    aa = np.arange(128)[None, :]
    b2 = np.arange(128)[:, None]
    n = 128 * aa + b2
    pos = np.where(n < L, n, 2 * L - n)
    pos = np.where(n == L, 0, pos).astype(np.float64).reshape(-1)
    tt = pos / (L - 1)
    angp = 2.0 * np.pi * pos / L
    bands = np.linspace(1e-4, 15.0, 16)
    z = np.concatenate([tt[None, :], np.cos(bands[:, None] * angp[None, :]),
                        -np.sin(bands[:, None] * angp[None, :])], axis=0)
    t['ztab'] = z.astype(NPBF)
    max_decay = math.log(1e-2) / 0.3
    min_decay = math.log(1e-2) / 1.5
    deltas = np.abs(np.linspace(min_decay, max_decay, 512))
    n3 = 128 * np.arange(128)[:, None] + np.arange(128)[None, :]
    pos3 = np.where(n3 < L, n3, 2 * L - n3).astype(np.float64) / (L - 1)
    sgn = np.where(n3 < L, 1.0, -1.0)
    sgn = np.where(n3 == L, 0.0, sgn)
    dk = np.exp(-pos3[:, None, :] * deltas[None, :, None]) * sgn[:, None, :]
    t['dtab'] = dk.reshape(128, 512 * 128).astype(NPBF)
    t['ident'] = np.eye(128).astype(NPBF)
    _TAB_CACHE['sh'] = t
    return t


def core_tables(h):
    key = ('c', h)
    if key in _TAB_CACHE:
        return _TAB_CACHE[key]
    sh = shared_tables()
    at = _atrue(h)
    t = {}
    f1 = sh['f1full']
    t['f1tab'] = np.concatenate([f1, np.concatenate([f1[at], np.zeros((64, 192))], axis=0)], axis=1).astype(NPBF)
    m = (np.arange(64, dtype=np.float64) + 0.5)
    n = 128.0 * at[None, None, :] + np.arange(128, dtype=np.float64)[None, :, None]
    psi = 2 * np.pi * m[:, None, None] * n / NFFT
    H = np.concatenate([np.cos(psi), -np.sin(psi)], axis=0) * (2.0 / NFFT)
    t['htab'] = H.reshape(128, 128 * 64).astype(NPBF)
    tpos = 4096 * h + np.arange(HL)
    inv = []
    for w in (2, 4, 8, 16):
        lo = np.clip(tpos - w // 2, 0, L - 1)
        hi = np.clip(tpos + (w - 1 - w // 2), 0, L - 1)
        inv.append(1.0 / (hi - lo + 1))
    t['invc'] = np.stack(inv, 0).astype(np.float32)
    _TAB_CACHE[key] = t
    return t


def seg_ext(xb, start):
    out = np.zeros((HL + 2 * HALO, DM), np.float32)
    lo, hi = start - HALO, start + HL + HALO
    slo, shi = max(lo, 0), min(hi, L)
    out[slo - lo:shi - lo] = xb[slo:shi]
    return out


WEIGHT_SPECS = [
    ('pre_norm_g', [1, DM]), ('w_in', [DM, 3072]), ('conv_w', [3, 1536]), ('conv_b', [1, 1536]),
    ('filt_w1', [33, 64]), ('filt_b1', [1, 64]), ('filt_w2', [64, 64]), ('filt_b2', [1, 64]),
    ('filt_w3', [64, 64]), ('filt_b3', [1, 64]), ('filt_freq', [1, 64]), ('filt_w_out', [64, 2048]),
    ('hyena_d', [2, 512]), ('pool_w', [4, 128, 128]), ('pool_scale', [1, 512]),
    ('norm_h_g', [1, 512]), ('norm_p_g', [1, 512]), ('w_out', [DM, DM]), ('post_norm_g', [1, DM]),
]
TABLE_SPECS = [
    ('gtab', [128, 16384], BF16), ('ctab', [128, 256], BF16), ('ztab', [33, 16384], BF16),
    ('dtab', [128, 65536], BF16), ('ident', [128, 128], BF16), ('f1tab', [128, 384], BF16),
    ('htab', [128, 8192], BF16), ('invc', [4, HL], F32),
]


def build(phases=('F', 'A', 'B', 'C'), debug=False):
    nc = bass.Bass("TRN2", target_bir_lowering=False)
    S = Sched(nc)
    T = {}
    T['xm'] = nc.dram_tensor("xm", [HL + 2 * HALO, DM], F32, kind="ExternalInput")
    T['xo'] = nc.dram_tensor("xo", [HL + 2 * HALO, DM], F32, kind="ExternalInput")
    for nm, shp in WEIGHT_SPECS:
        T[nm] = nc.dram_tensor(nm, shp, F32, kind="ExternalInput")
    for nm, shp, dt in TABLE_SPECS:
        T[nm] = nc.dram_tensor(nm, shp, dt, kind="ExternalInput")
    out = nc.dram_tensor("out", [HL, DM], F32, kind="ExternalOutput")
    sk = "ExternalOutput" if debug else "Internal"
    v_s = nc.dram_tensor("v_s", [512, L], BF16, kind=sk)
    x1_s = nc.dram_tensor("x1_s", [512, L], BF16, kind=sk)
    x2_s = nc.dram_tensor("x2_s", [512, HL], BF16, kind=sk)
    szh_s = nc.dram_tensor("szh_s", [512, HL], BF16, kind=sk)
    szp_s = nc.dram_tensor("szp_s", [512, HL], BF16, kind=sk)
    ypr_s = nc.dram_tensor("ypr_s", [512, HL], BF16, kind=sk)
    c2_s = nc.dram_tensor("c2_s", [512, HL], BF16, kind=sk)
    y1_s = nc.dram_tensor("y1_s", [512, HL], BF16, kind=sk)
    k_s = nc.dram_tensor("k_s", [2 * NG, 128, 8192], BF16, kind=sk)

    ARENA = 204 * 1024
    arena = nc.alloc_sbuf_tensor("arena", [128, ARENA], U8)
    st = {'off': 0}

    def sb(shape, dt, parts=128):
        n = int(np.prod(shape)) * (4 if dt == F32 else 2)
        o = st['off']
        assert o + n <= ARENA, (o, n)
        v = arena[0:parts, o:o + n].bitcast(dt)
        st['off'] = (o + n + 63) // 64 * 64
        return v

    NPB = 6
    psb = [nc.alloc_psum_tensor("psb%d" % i, [128, 512], F32) for i in range(NPB)]
    pst = nc.alloc_psum_tensor("pst", [128, 1024], BF16)
    pst2 = nc.alloc_psum_tensor("pst2", [128, 1024], BF16)
    prr = {'i': 0}

    def psum():
        i = prr['i']
        prr['i'] = (i + 1) % NPB
        return psb[i], 'psb%d' % i

    def v3(ap, **kw):
        names = list(kw.keys())
        pat = "p (" + " ".join(names) + ") -> p " + " ".join(names)
        return ap.rearrange(pat, **kw)

    gtab = sb([16384], BF16)
    S.dma(lambda e: e.dma_start(out=gtab, in_=T['gtab'].ap()), writes=['gtab'])
    G4 = gtab.rearrange("p (c t d) -> p c t d", c=64, t=2)
    f1t = sb([384], BF16)
    S.dma(lambda e: e.dma_start(out=f1t, in_=T['f1tab'].ap()), writes=['f1t'])
    base_off = st['off']

    def fft_fwd(src_fn, src_reg, K, f1ap, nch, A_sb, U_sb, areg, ureg, after_bank=None):
        A4 = A_sb.rearrange("p (c q n) -> p c q n", c=64, q=3)
        A3 = A_sb.rearrange("p (c x) -> p c x", c=64)
        aregs = ['%s_%d' % (areg, k) for k in range(nch // 2)]
        sregs = src_reg if isinstance(src_reg, list) else [src_reg]
        for ch0 in range(0, nch, 2):
            ps, pr = psum()
            for i in range(2):
                rr = src_reg(ch0 + i) if callable(src_reg) else sregs
                S.op('pe', lambda e, ps=ps, i=i, ch=ch0 + i: e.matmul(
                    ps[:, i * 192:(i + 1) * 192], lhsT=src_fn(ch), rhs=f1ap, start=True, stop=True),
                    reads=rr + ['f1t'], writes=[pr], quiet=(i == 0))
            eng = 'act' if (ch0 // 2) % 2 == 0 else 'dve'
            src = ps[:, 0:384].rearrange("p (i q c) -> p c q i", i=2, q=3)
            dst = A4[:, :, :, ch0:ch0 + 2]
            if eng == 'act':
                S.op('act', lambda e, s=src, d=dst: e.activation(out=d, in_=s, func=AF.Copy),
                     reads=[pr], writes=[aregs[ch0 // 2]])
            else:
                S.op('dve', lambda e, s=src, d=dst: e.tensor_copy(out=d, in_=s),
                     reads=[pr], writes=[aregs[ch0 // 2]])
        U3 = U_sb.rearrange("p (n c) -> p n c", c=64)
        cpb = 512 // (2 * nch)
        for c0 in range(0, 64, cpb):
            ps, pr = psum()
            for cc in range(cpb):
                c = c0 + cc
                S.op('pe', lambda e, ps=ps, c=c, cc=cc: e.matmul(
                    ps[:, cc * 2 * nch:(cc + 1) * 2 * nch], lhsT=G4[:, c, 0, :], rhs=A3[:, c, 0:2 * nch], start=True, stop=False),
                    reads=aregs + ['gtab'], writes=[pr], quiet=True)
                S.op('pe', lambda e, ps=ps, c=c, cc=cc: e.matmul(
                    ps[:, cc * 2 * nch:(cc + 1) * 2 * nch], lhsT=G4[:, c, 1, :], rhs=A3[:, c, nch:3 * nch], start=False, stop=True),
                    reads=aregs + ['gtab'], writes=[pr], quiet=(cc < cpb - 1))
            src = ps[:, 0:cpb * 2 * nch].rearrange("p (cc x) -> p x cc", cc=cpb)
            ur = ureg(c0)
            if (c0 // cpb) % 2 == 0 or after_bank is not None:
                S.op('act', lambda e, src=src, c0=c0: e.activation(out=U3[:, :, c0:c0 + cpb], in_=src, func=AF.Copy),
                     reads=[pr], writes=[ur])
            else:
                S.op('dve', lambda e, src=src, c0=c0: e.tensor_copy(out=U3[:, :, c0:c0 + cpb], in_=src),
                     reads=[pr], writes=[ur])
            if after_bank is not None:
                after_bank(c0 + cpb)

    if 'F' in phases:
        st['off'] = base_off
        h3s = sb([16384], BF16)
        wps = sb([1024], BF16)
        wpf = sb([1024], F32)
        zt = sb([512], BF16, parts=33)
        w1f = sb([64], F32, parts=33)
        w1b = sb([64], BF16, parts=33)
        w2f = sb([64], F32, parts=64)
        w2b = sb([64], BF16, parts=64)
        w3f = sb([64], F32, parts=64)
        w3b = sb([128], BF16, parts=64)
        prm = sb([8], F32)
        m2pi = sb([1], F32)
        u_t = sb([512], F32)
        r_t = sb([512], F32)
        h1t = sb([512], BF16, parts=64)
        h2t = sb([512], BF16, parts=64)
        kf = sb([2 * CG * 128], BF16)
        d_sb = sb([CG * 128], BF16)
        A_f = sb([64 * 3 * CG], BF16)
        K_f = sb([2 * CG * 64], BF16)

        S.dma(lambda e: e.dma_start(out=w1f, in_=T['filt_w1'].ap()), writes=['w1f'])
        S.dma(lambda e: e.dma_start(out=w2f, in_=T['filt_w2'].ap()), writes=['w2f'])
        S.dma(lambda e: e.dma_start(out=w3f, in_=T['filt_w3'].ap()), writes=['w3f'])
        for half in range(2):
            for j, nm in enumerate(('filt_freq', 'filt_b1', 'filt_b2', 'filt_b3')):
                S.dma(lambda e, half=half, j=j, nm=nm: e.dma_start(
                    out=prm[64 * half:64 * half + 64, j:j + 1], in_=AP(T[nm], 0, [[1, 64], [1, 1]])),
                    writes=['prm'])
            S.dma(lambda e, half=half: e.dma_start(
                out=wpf[64 * half:64 * half + 64, :].rearrange("p (o c) -> p o c", o=2),
                in_=AP(T['filt_w_out'], 512 * half, [[2048, 64], [1024, 2], [1, 512]])), writes=['wpf'])
        S.op('dve', lambda e: e.tensor_copy(out=wps, in_=wpf), reads=['wpf'], writes=['wps'])
        S.op('dve', lambda e: e.tensor_copy(out=w1b, in_=w1f), reads=['w1f'], writes=['w1b'])
        S.op('dve', lambda e: e.tensor_copy(out=w2b, in_=w2f), reads=['w2f'], writes=['w2b'])
        S.op('dve', lambda e: e.tensor_copy(out=w3b[:, 0:64], in_=w3f), reads=['w3f'], writes=['w3b'])
        S.op('dve', lambda e: e.tensor_copy(out=w3b[:, 64:128], in_=w3f), reads=['w3f'], writes=['w3b'])
        S.op('dve', lambda e: e.memset(m2pi, -2.0 * math.pi), writes=['m2pi'])
        S.op('dve', lambda e: e.tensor_scalar(out=prm[:, 4:5], in0=prm[:, 0:1], scalar1=1.0 / (2 * math.pi), scalar2=0.0,
                                              op0=ALU.mult, op1=ALU.add), reads=['prm'], writes=['prm'])
        for j in range(3):
            S.op('dve', lambda e, j=j: e.tensor_tensor(out=prm[:, 5 + j:6 + j], in0=prm[:, 1 + j:2 + j], in1=prm[:, 4:5],
                                                       op=ALU.mult), reads=['prm'], writes=['prm'])

        NQ = 4
        ZT = [zt] + [sb([512], BF16, parts=33) for _ in range(NQ - 1)]
        UT = [u_t] + [sb([512], F32) for _ in range(NQ - 1)]
        RT = [r_t] + [sb([512], F32) for _ in range(NQ - 1)]
        H1 = [h1t] + [sb([512], BF16, parts=64) for _ in range(NQ - 1)]
        H2 = [h2t] + [sb([512], BF16, parts=64) for _ in range(NQ - 1)]

        def sin_layers(lhs, lreg, np_, fbcol, rhs_fn, rreg_fn, out_fn, oreg_fn):
            pss = []
            for q in range(NQ):
                ps, pr = psum()
                pss.append((ps, pr))
                S.op('pe', lambda e, ps=ps, q=q: e.matmul(ps[0:np_, :], lhsT=lhs, rhs=rhs_fn(q), start=True, stop=True),
                     reads=[lreg, rreg_fn(q)], writes=[pr])
            for q in range(NQ):
                ps, pr = pss[q]
                S.op('act', lambda e, ps=ps, q=q: e.activation(out=UT[q][0:np_, :], in_=ps[0:np_, :], func=AF.Identity, scale=prm[0:np_, 4:5],
                                                              bias=prm[0:np_, fbcol:fbcol + 1]), reads=[pr, 'prm'], writes=['u_t%d' % q])
            for q in range(NQ):
                S.op('dve', lambda e, q=q: e.scalar_tensor_tensor(out=RT[q][0:np_, :], in0=UT[q][0:np_, :], scalar=-0.5, in1=UT[q][0:np_, :],
                                                                op0=ALU.is_lt, op1=ALU.add), reads=['u_t%d' % q], writes=['r_t%d' % q])
            for q in range(NQ):
                S.op('dve', lambda e, q=q: e.scalar_tensor_tensor(out=UT[q][0:np_, :], in0=RT[q][0:np_, :], scalar=0.5, in1=RT[q][0:np_, :],
                                                                op0=ALU.is_le, op1=ALU.add), reads=['r_t%d' % q], writes=['u_t%d' % q])
            for q in range(NQ):
                S.op('act', lambda e, q=q: e.activation(out=out_fn(q), in_=UT[q][0:np_, :], func=AF.Sin, scale=2 * math.pi,
                                                       bias=m2pi[0:np_, 0:1]), reads=['u_t%d' % q, 'm2pi'], writes=[oreg_fn(q)])

        for tj in range(0, 32, NQ):
            for q in range(NQ):
                ti = tj + q
                S.dma(lambda e, ti=ti, q=q: e.dma_start(out=ZT[q], in_=T['ztab'].ap()[:, ti * 512:(ti + 1) * 512]), writes=['zt%d' % q])
            sin_layers(w1b, 'w1b', 64, 5, lambda q: ZT[q], lambda q: 'zt%d' % q, lambda q: H1[q], lambda q: 'h1t%d' % q)
            sin_layers(w2b, 'w2b', 64, 6, lambda q: H1[q], lambda q: 'h1t%d' % q, lambda q: H2[q], lambda q: 'h2t%d' % q)
            sin_layers(w3b, 'w3b', 128, 7, lambda q: H2[q], lambda q: 'h2t%d' % q,
                       lambda q, tj=tj: h3s[:, (tj + q) * 512:(tj + q + 1) * 512], lambda q: 'h3s')
        h3v = h3s.rearrange("p (b a) -> p b a", a=128)
        S.op('pool', lambda e: e.memset(h3v[0:64, :, 64:128], 0.0), reads=['h3s'], writes=['h3s'])
        S.op('pool', lambda e: e.memset(h3v[64:128, :, 0:64], 0.0), reads=['h3s'], writes=['h3s'])

        wps3 = wps.rearrange("p (o c) -> p o c", o=2)
        kf4 = kf.rearrange("p (o c b) -> p o c b", o=2, c=CG)
        d3 = d_sb.rearrange("p (c b) -> p c b", c=CG)
        for g in range(int(_os.environ.get('FNG', NG))):
            S.dma(lambda e, g=g: e.dma_start(out=d_sb, in_=T['dtab'].ap()[:, g * CG * 128:(g + 1) * CG * 128]),
                  writes=['d_sb'])
            for bb in range(32):
                ps, pr = psum()
                for j in range(4):
                    b = 4 * bb + j
                    S.op('pe', lambda e, ps=ps, j=j, b=b, g=g: e.matmul(
                        ps[:, j * 128:(j + 1) * 128].rearrange("p (o c) -> p o c", o=2),
                        lhsT=h3s[:, b * 128:(b + 1) * 128], rhs=wps3[:, :, g * CG:(g + 1) * CG], start=True, stop=True),
                        reads=['h3s', 'wps'], writes=[pr], quiet=(j < 3))
                for o in range(2):
                    src = ps.rearrange("p (j o c) -> p o c j", j=4, o=2)[:, o]
                    S.op('dve', lambda e, src=src, o=o, bb=bb: e.tensor_tensor(
                        out=kf4[:, o, :, 4 * bb:4 * bb + 4], in0=src, in1=d3[:, :, 4 * bb:4 * bb + 4], op=ALU.mult),
                        reads=[pr, 'd_sb'], writes=['kf%d_%d' % (o, bb)])
            for o in range(2):
                fft_fwd(lambda ch, o=o: kf4[:, o, ch, :], ['kf%d_%d' % (o, bb_) for bb_ in range(32)], 128, f1t[:, 0:192], CG, A_f, K_f,
                        'A_f', lambda c0: 'K_f%d' % (c0 // 4))
                S.dma(lambda e, o=o, g=g: e.dma_start(out=k_s.ap()[o * NG + g], in_=K_f), reads=['K_f%d' % q_ for q_ in range(16)],
                      writes=['k_s%d' % (o * NG + g)])
        S.barrier()

    if 'A' in phases:
        st['off'] = base_off
        winb = sb([8 * 3072], BF16)
        wst = sb([3072], F32)
        gbc = sb([DM], F32)
        ident = sb([128], BF16)
        xt = [sb([3 * DM], F32) for _ in range(2)]
        xn = [sb([3 * DM], BF16) for _ in range(2)]
        junk = sb([DM], BF16)
        ssq = [sb([4], F32) for _ in range(2)]
        rstd = [sb([4], F32) for _ in range(2)]
        epsb = sb([1], F32)
        hT = [sb([8 * TW], BF16) for _ in range(2)]
        cwt = sb([4 * 12], F32)
        psc = sb([4], F32)
        pwf = sb([4 * 128], F32)
        pwb = sb([4 * 128], BF16)
        tA = [sb([256], F32) for _ in range(2)]
        tB = [sb([256], F32) for _ in range(2)]
        NOB = 4
        ob = [sb([256], BF16) for _ in range(NOB)]
        u_sbs = [sb([TW], F32) for _ in range(4)]
        s_as = [sb([TW], F32) for _ in range(4)]
        s_bs = [sb([TW], F32) for _ in range(4)]
        tPs = [sb([256], F32) for _ in range(4)]
        icbs = [[sb([256], F32) for _ in range(2)] for _ in range(4)]
        pooleds = [sb([256], BF16) for _ in range(4)]

        for kc in range(8):
            S.dma(lambda e, kc=kc: e.dma_start(out=wst, in_=T['w_in'].ap()[kc * 128:(kc + 1) * 128, :]), writes=['wst'])
            eng = 'dve' if kc % 2 == 0 else 'pool'
            S.op(eng, lambda e, kc=kc: e.tensor_copy(out=winb[:, kc * 3072:(kc + 1) * 3072], in_=wst),
                 reads=['wst'], writes=['winb'])
        S.dma(lambda e: e.dma_start(out=gbc, in_=AP(T['pre_norm_g'], 0, [[0, 128], [1, DM]])), writes=['gbc'])
        S.dma(lambda e: e.dma_start(out=ident, in_=T['ident'].ap()), writes=['ident'])
        cl_regs = []

        def colload(dst, col, src_t, off):
            cl_regs.append('colload%d' % len(cl_regs))
            S.dma(lambda e: e.dma_start(out=dst[:, col:col + 1], in_=AP(src_t, off, [[1, 128], [1, 1]])), writes=[cl_regs[-1]])
        for j in range(3):
            for m in range(12):
                colload(cwt, j * 12 + m, T['conv_w'], j * 1536 + m * 128)
        for m in range(12):
            colload(cwt, 36 + m, T['conv_b'], m * 128)
        for g in range(4):
            colload(psc, g, T['pool_scale'], g * 128)
        S.op('dve', lambda e: e.tensor_copy(out=cwt, in_=cwt), reads=list(cl_regs), writes=['cwt'])
        S.op('dve', lambda e: e.tensor_copy(out=psc, in_=psc), reads=list(cl_regs), writes=['psc'])
        S.dma(lambda e: e.dma_start(out=pwf.rearrange("p (g d) -> p g d", g=4),
                                    in_=AP(T['pool_w'], 0, [[128, 128], [128 * 128, 4], [1, 128]])), writes=['pwf'])
        S.op('dve', lambda e: e.tensor_copy(out=pwb, in_=pwf), reads=['pwf'], writes=['pwb'])
        S.op('dve', lambda e: e.memset(epsb, EPS), writes=['epsb'])
        winb3 = winb.rearrange("p (k n) -> p k n", k=8)
        hT3 = [h_.rearrange("p (k t) -> p k t", k=8) for h_ in hT]
        xt3 = [x_.rearrange("p (s d) -> p s d", s=3) for x_ in xt]
        xn3 = [x_.rearrange("p (s d) -> p s d", s=3) for x_ in xn]
        psts = [pst, pst2]
        obi = [0]
        tci = [0]

        def store(dst_t, row0, col0, src, sreg):
            S.dma(lambda e: e.dma_start(out=dst_t.ap()[row0:row0 + 128, col0:col0 + 256], in_=src),
                  reads=[sreg], writes=[dst_t.name + "_%d" % (row0 // 128)])

        NTT = 2 * NT

        def tinfo(i):
            seg, ti = divmod(i, NT)
            return seg, ti, (T['xm'] if seg == 0 else T['xo']), (list(range(24)) if seg == 0 else list(range(8)))

        def f_load(i):
            seg, ti, xsrc, _ = tinfo(i)
            p = i % 2
            r0 = 256 * ti
            S.dma(lambda e: e.dma_start(out=xt3[p][:, 0:2, :], in_=xsrc.ap()[r0:r0 + 256, :].rearrange("(s p) d -> p s d", p=128)),
                  writes=['xt%d' % p])
            S.dma(lambda e: e.dma_start(out=xt3[p][0:16, 2, :], in_=xsrc.ap()[r0 + 256:r0 + 272, :]), writes=['xt%d' % p])

        def f_norm(i):
            p = i % 2
            S.op('dve', lambda e: e.memset(ssq[p], 0.0), writes=['ssq%d' % p])
            for s_ in range(3):
                P_ = 128 if s_ < 2 else 16
                S.op('act', lambda e, s_=s_, P_=P_: e.activation(out=junk[0:P_, :], in_=xt3[p][0:P_, s_, :], func=AF.Square,
                                                                accum_out=ssq[p][0:P_, s_:s_ + 1]),
                     reads=['xt%d' % p], writes=['junk', 'ssq%d' % p])
            S.op('act', lambda e: e.activation(out=rstd[p][:, 0:3], in_=ssq[p][:, 0:3], func=AF.Sqrt, scale=1.0 / DM,
                                               bias=epsb[:, 0:1]), reads=['ssq%d' % p, 'epsb'], writes=['rstd%d' % p])
            S.op('dve', lambda e: e.reciprocal(out=rstd[p][:, 0:3], in_=rstd[p][:, 0:3]), reads=['rstd%d' % p], writes=['rstd%d' % p])
            for s_ in range(3):
                P_ = 128 if s_ < 2 else 16
                S.op('dve', lambda e, s_=s_, P_=P_: e.scalar_tensor_tensor(
                    out=xn3[p][0:P_, s_, :], in0=xt3[p][0:P_, s_, :], scalar=rstd[p][0:P_, s_:s_ + 1], in1=gbc[0:P_, :],
                    op0=ALU.mult, op1=ALU.mult), reads=['xt%d' % p, 'rstd%d' % p, 'gbc'], writes=['xn%d' % p])

        def f_T(i):
            p = i % 2
            for s_ in range(3):
                P_ = 128 if s_ < 2 else 16
                pt_ = psts[s_ % 2]
                preg = 'pst%d' % (s_ % 2)
                for kc in range(8):
                    S.op('pe', lambda e, s_=s_, kc=kc, P_=P_, pt_=pt_: e.transpose(
                        pt_[:, kc * 128:kc * 128 + P_], xn3[p][0:P_, s_, kc * 128:(kc + 1) * 128], ident[0:P_, 0:P_]),
                        reads=['xn%d' % p, 'ident'], writes=[preg], quiet=(kc < 7))
                S.op('dve', lambda e, s_=s_, P_=P_, pt_=pt_: e.tensor_copy(
                    out=hT3[p][:, :, s_ * 128:s_ * 128 + P_], in_=pt_[:, :].rearrange("p (k t) -> p k t", k=8)[:, :, 0:P_]),
                    reads=[preg], writes=['hT%d' % p])

        def back_chunk(i, m):
            seg, ti, _, _ = tinfo(i)
            p = i % 2
            n0 = HL * seg + 256 * ti
            ps, pr = psum()
            for kc in range(8):
                S.op('pe', lambda e, ps=ps, kc=kc: e.matmul(
                    ps[:, 0:TW], lhsT=winb3[:, kc, m * 128:(m + 1) * 128], rhs=hT3[p][:, kc, :],
                    start=(kc == 0), stop=(kc == 7)), reads=['winb', 'hT%d' % p], writes=[pr], quiet=(kc < 7))
            if m < 12:
                o_ = ob[obi[0] % NOB]
                oreg = 'ob%d' % (obi[0] % NOB)
                obi[0] += 1
                q_ = tci[0] % 2
                tci[0] += 1
                tA_, tB_ = tA[q_], tB[q_]
                S.op('act', lambda e: e.activation(
                    out=tA_, in_=ps[:, 8:264], func=AF.Identity, scale=cwt[:, 12 + m:13 + m], bias=cwt[:, 36 + m:37 + m]),
                    reads=[pr, 'cwt'], writes=['tA%d' % q_])
                S.op('dve', lambda e: e.scalar_tensor_tensor(
                    out=tB_, in0=ps[:, 7:263], scalar=cwt[:, m:m + 1], in1=tA_, op0=ALU.mult, op1=ALU.add),
                    reads=[pr, 'cwt', 'tA%d' % q_], writes=['tB%d' % q_])
                S.op('dve', lambda e: e.scalar_tensor_tensor(
                    out=o_, in0=ps[:, 9:265], scalar=cwt[:, 24 + m:25 + m], in1=tB_, op0=ALU.mult, op1=ALU.add),
                    reads=[pr, 'cwt', 'tB%d' % q_], writes=[oreg])
                if m < 4:
                    store(v_s, m * 128, n0, o_, oreg)
                elif m < 8:
                    store(x1_s, (m - 4) * 128, n0, o_, oreg)
                else:
                    store(x2_s, (m - 8) * 128, n0, o_, oreg)
            elif m < 16 or m >= 20:
                o_ = ob[obi[0] % NOB]
                oreg = 'ob%d' % (obi[0] % NOB)
                obi[0] += 1
                S.op('act', lambda e: e.activation(out=o_, in_=ps[:, 8:264], func=AF.Silu), reads=[pr], writes=[oreg])
                if m < 16:
                    store(szh_s, (m - 12) * 128, n0, o_, oreg)
                else:
                    store(szp_s, (m - 20) * 128, n0, o_, oreg)
            else:
                g = m - 16
                w = 2 << g
                u_sb, s_a, s_b, tP, icb, pooled = u_sbs[g], s_as[g], s_bs[g], tPs[g], icbs[g][i % 2], pooleds[g]
                S.op('act', lambda e: e.activation(out=u_sb, in_=ps[:, 0:TW], func=AF.Copy), reads=[pr], writes=['u_sb%d' % g])
                cur, creg, ln = u_sb, 'u_sb%d' % g, TW
                for k in range(g + 1):
                    sh = 1 << k
                    nxt, nreg = (s_a, 's_a%d' % g) if k % 2 == 0 else (s_b, 's_b%d' % g)
                    S.op('pool', lambda e, cur=cur, nxt=nxt, ln=ln, sh=sh: e.tensor_tensor(
                        out=nxt[:, 0:ln - sh], in0=cur[:, 0:ln - sh], in1=cur[:, sh:ln], op=ALU.add),
                        reads=[creg], writes=[nreg])
                    cur, creg, ln = nxt, nreg, ln - sh
                c0 = 8 - w // 2
                S.op('pool', lambda e, cur=cur: e.tensor_tensor(
                    out=tP, in0=cur[:, c0:c0 + 256], in1=icb, op=ALU.mult), reads=[creg, 'icb%d_%d' % (g, i % 2)], writes=['tP%d' % g])
                S.op('pool', lambda e: e.tensor_tensor(out=pooled, in0=tP, in1=u_sb[:, 8:264], op=ALU.subtract),
                     reads=['tP%d' % g, 'u_sb%d' % g], writes=['pooled%d' % g])

        def pool_part2(i, g):
            seg, ti, _, _ = tinfo(i)
            n0 = HL * seg + 256 * ti
            pooled = pooleds[g]
            ps2, pr2 = psum()
            S.op('pe', lambda e: e.matmul(ps2[:, 0:256], lhsT=pwb[:, g * 128:(g + 1) * 128], rhs=pooled,
                                          start=True, stop=True), reads=['pwb', 'pooled%d' % g], writes=[pr2])
            o_ = ob[obi[0] % NOB]
            oreg = 'ob%d' % (obi[0] % NOB)
            obi[0] += 1
            S.op('act', lambda e: e.activation(out=o_, in_=ps2[:, 0:256], func=AF.Copy, scale=psc[:, g:g + 1]),
                 reads=[pr2, 'psc'], writes=[oreg])
            store(ypr_s, g * 128, n0, o_, oreg)

        def icb_loads(i):
            seg, ti, _, _ = tinfo(i)
            n0 = 256 * ti
            for g in range(4):
                S.dma(lambda e, g=g: e.dma_start(out=icbs[g][i % 2], in_=AP(T['invc'], g * HL + n0, [[0, 128], [1, 256]])),
                      writes=['icb%d_%d' % (g, i % 2)])

        f_load(0)
        f_load(1)
        f_norm(0)
        f_T(0)
        for i in range(NTT):
            chunks = tinfo(i)[3]
            if len(chunks) > 8:
                zs = [12, 13, 14, 15, 20, 21, 22, 23]
                order = [16, 17, 18, 19]
                for q_ in range(6):
                    order += [2 * q_, 2 * q_ + 1, zs[q_]]
                order += zs[6:]
            else:
                order = list(chunks)
            tpos = (2 * len(order)) // 3
            if i == 0:
                icb_loads(0)
            if i + 1 < NT:
                icb_loads(i + 1)
            p2at = {9: 0, 12: 1, 15: 2, 18: 3} if len(chunks) > 8 else {}
            for idx, m in enumerate(order):
                if idx in p2at:
                    pool_part2(i, p2at[idx])
                if idx == 0 and i + 2 < NTT:
                    f_load(i + 2)
                if idx == 2 and i + 1 < NTT:
                    f_norm(i + 1)
                if idx == tpos and i + 1 < NTT:
                    f_T(i + 1)
                back_chunk(i, m)
            if len(chunks) > 8 and False:
                pass
        S.barrier()

    if 'B' in phases:
        st['off'] = base_off
        htab = sb([8192], BF16)
        ctab = sb([256], BF16)
        dbc = sb([1024], F32, parts=64)
        v_sb = sb([CG * 128], BF16, parts=64)
        x1_sb = sb([CG * 128], BF16, parts=64)
        K_sb = [sb([2 * CG * 64], BF16) for _ in range(2)]
        A_b = sb([64 * 3 * CG], BF16)
        AY = sb([64 * 3 * CG], BF16)
        UZ = sb([2 * CG * 64], BF16)
        T1 = sb([CG * 16], BF16)
        T2 = sb([CG * 16], BF16)
        T3 = sb([CG * 16], BF16)
        T4 = sb([CG * 16], BF16)
        hbs = [sb([512], F32, parts=64) for _ in range(3)]
        S.dma(lambda e: e.dma_start(out=htab, in_=T['htab'].ap()), writes=['htab'])
        S.dma(lambda e: e.dma_start(out=ctab, in_=T['ctab'].ap()), writes=['ctab'])
        S.dma(lambda e: e.dma_start(out=dbc, in_=AP(T['hyena_d'], 0, [[0, 64], [1, 1024]])), writes=['dbc'])
        H3 = htab.rearrange("p (b a) -> p b a", a=64)
        v3_ = v_sb.rearrange("p (c b) -> p c b", c=CG)
        x13 = x1_sb.rearrange("p (c b) -> p c b", c=CG)
        U4 = UZ.rearrange("p (q n c) -> p q n c", q=2, n=CG)
        Y4 = AY.rearrange("p (n q c) -> p n q c", n=CG, q=3)
        Z3 = UZ.rearrange("p (b n) -> p b n", n=CG)
        Y3 = AY.rearrange("p (n x) -> p n x", n=CG)
        T13 = T1.rearrange("p (n c) -> p n c", c=16)
        T23 = T2.rearrange("p (n c) -> p n c", c=16)
        T33 = T3.rearrange("p (n c) -> p n c", c=16)
        T43 = T4.rearrange("p (n c) -> p n c", c=16)
        UB = ['Ub%d' % q_ for q_ in range(16)]
        ZB = ['Zb%d' % q_ for q_ in range(16)]
        X1 = ['x1_sb%d' % q_ for q_ in range(16)]
        VS = ['v_sb%d' % i_ for i_ in range(16)]
        YQ = ['Yq0', 'Yq1', 'Yq2', 'Yq3']

        def load_tc(dst3, src_t, g, reg):
            for q4 in range(4):
                S.dma(lambda e, q4=q4: e.dma_start(out=dst3[:, 16 * q4:16 * q4 + 16, :],
                                                   in_=AP(src_t, (g * CG + 16 * q4) * L, [[128, 64], [L, 16], [1, 128]])), writes=(reg if isinstance(reg, list) else [reg]))

        def spectral_q(o, cq):
            K4 = K_sb[o].rearrange("p (q n c) -> p q n c", q=2, n=CG)
            kr = 'K_sb%d' % o
            cs = slice(cq * 16, (cq + 1) * 16)
            ur, yr = None, YQ[cq]
            urs = UB[4 * cq:4 * cq + 4]
            S.op('dve', lambda e: e.tensor_tensor(out=T13, in0=U4[:, 0, :, cs], in1=K4[:, 0, :, cs], op=ALU.mult), reads=urs + [kr], writes=['T1'])
            S.op('dve', lambda e: e.tensor_tensor(out=T23, in0=U4[:, 1, :, cs], in1=K4[:, 1, :, cs], op=ALU.mult), reads=urs + [kr], writes=['T2'])
            S.op('dve', lambda e: e.tensor_tensor(out=T33, in0=U4[:, 0, :, cs], in1=K4[:, 1, :, cs], op=ALU.mult), reads=urs + [kr], writes=['T3'])
            S.op('dve', lambda e: e.tensor_tensor(out=T43, in0=U4[:, 1, :, cs], in1=K4[:, 0, :, cs], op=ALU.mult), reads=urs + [kr], writes=['T4'])
            S.op('dve', lambda e: e.tensor_tensor(out=Y4[:, :, 0, cs], in0=T13, in1=T23, op=ALU.subtract), reads=['T1', 'T2'], writes=[yr])
            S.op('dve', lambda e: e.tensor_tensor(out=Y4[:, :, 2, cs], in0=T23, in1=T13, op=ALU.subtract), reads=['T1', 'T2'], writes=[yr])
            S.op('dve', lambda e: e.tensor_tensor(out=Y4[:, :, 1, cs], in0=T33, in1=T43, op=ALU.add), reads=['T3', 'T4'], writes=[yr])

        def conv_fwd(o):
            def ab(cend):
                if cend % 16 == 0:
                    spectral_q(o, cend // 16 - 1)
            fft_fwd(lambda ch: v3_[:, ch, :], VS, 64, f1t[0:64, 192:384], CG, A_b, UZ, 'A_b',
                    lambda c0: UB[c0 // 4], after_bank=ab)

        def fft_inv(handler):
            for ch0 in range(0, CG, 4):
                ps, pr = psum()
                for i in range(4):
                    ch = ch0 + i
                    S.op('pe', lambda e, ps=ps, i=i, ch=ch: e.matmul(
                        ps[:, i * 128:(i + 1) * 128], lhsT=Y3[:, ch, 0:128], rhs=ctab[:, 0:128], start=True, stop=False),
                        reads=YQ + ['ctab'], writes=[pr], quiet=True)
                    S.op('pe', lambda e, ps=ps, i=i, ch=ch: e.matmul(
                        ps[:, i * 128:(i + 1) * 128], lhsT=Y3[:, ch, 64:192], rhs=ctab[:, 128:256], start=False, stop=True),
                        reads=YQ + ['ctab'], writes=[pr], quiet=(i < 3))
                src = ps.rearrange("p (i b) -> p b i", i=4)
                if (ch0 // 4) % 2 == 0:
                    S.op('act', lambda e, src=src, ch0=ch0: e.activation(out=Z3[:, :, ch0:ch0 + 4], in_=src, func=AF.Copy),
                         reads=[pr], writes=[ZB[ch0 // 4]])
                else:
                    S.op('dve', lambda e, src=src, ch0=ch0: e.tensor_copy(out=Z3[:, :, ch0:ch0 + 4], in_=src),
                         reads=[pr], writes=[ZB[ch0 // 4]])
            for bg in range(16):
                ps, pr = psum()
                for j in range(8):
                    b = 8 * bg + j
                    S.op('pe', lambda e, ps=ps, j=j, b=b: e.matmul(
                        ps[0:64, j * CG:(j + 1) * CG], lhsT=H3[:, b, :], rhs=Z3[:, b, :], start=True, stop=True),
                        reads=['htab'] + ZB, writes=[pr], quiet=(j < 7))
                handler(bg, ps[0:64, :].rearrange("p (j n) -> p n j", j=8), pr)

        import os
        BSTOP = int(os.environ.get('BSTOP', '99'))
        for g in range(int(os.environ.get('BNG', NG))):
            if g == 0:
                load_tc(v3_, v_s, g, VS)
            load_tc(x13, x1_s, g, X1)
            for o in range(2):
                S.dma(lambda e, o=o, g=g: e.dma_start(out=K_sb[o], in_=k_s.ap()[o * NG + g]), writes=['K_sb%d' % o])
            if BSTOP < 1:
                continue
            conv_fwd(0)
            S.op('dve', lambda e, g=g: e.tensor_tensor(
                out=v3_, in0=v3_, in1=dbc[:, g * CG:(g + 1) * CG].unsqueeze(2).broadcast_to([64, CG, 128]), op=ALU.mult),
                reads=VS + ['dbc'], writes=VS)
            def h1(bg, psv, pr):
                hb3 = hbs[bg % 3].rearrange("p (n j) -> p n j", j=8)
                hr = 'hb%d' % (bg % 3)
                S.op('dve', lambda e: e.tensor_tensor(out=hb3, in0=psv, in1=v3_[:, :, 8 * bg:8 * bg + 8], op=ALU.add),
                     reads=[pr, 'v_sb%d' % bg], writes=[hr])
                S.op('pool', lambda e: e.tensor_tensor(out=v3_[:, :, 8 * bg:8 * bg + 8], in0=hb3, in1=x13[:, :, 8 * bg:8 * bg + 8],
                                                       op=ALU.mult), reads=[hr, 'x1_sb%d' % bg], writes=['v_sb%d' % bg])
            if BSTOP < 4:
                continue
            fft_inv(h1)
            if BSTOP < 5:
                continue
            S.dma(lambda e, g=g: e.dma_start(out=AP(y1_s, g * CG * HL, [[128, 32], [HL, CG], [1, 128]]), in_=v3_[0:32]),
                  reads=VS, writes=['y1_s'])
            conv_fwd(1)
            if g + 1 < NG:
                load_tc(v3_, v_s, g + 1, VS)

            def h2(bg, psv, pr):
                S.op('act', lambda e: e.activation(out=x13[:, :, 8 * bg:8 * bg + 8], in_=psv, func=AF.Copy),
                     reads=[pr], writes=['x1_sb%d' % bg])
            fft_inv(h2)
            S.dma(lambda e, g=g: e.dma_start(out=AP(c2_s, g * CG * HL, [[128, 32], [HL, CG], [1, 128]]), in_=x13[0:32]),
                  reads=X1, writes=['c2_s'])
        S.barrier()

    if 'C' in phases:
        st['off'] = base_off
        woutb = sb([8 * DM], BF16)
        wst2 = sb([DM], F32)
        pgbc = sb([DM], F32)
        cst = sb([12], F32)
        ones = sb([1], BF16)
        epsc = sb([1], F32)
        LDN = ('c2', 'y1', 'x2', 'szh', 'ypr', 'szp')
        ld = [{nm: sb([4 * 256], BF16) for nm in LDN} for _ in range(2)]
        xr = [sb([2 * DM], F32) for _ in range(2)]
        t1s = [sb([256], F32) for _ in range(4)]
        t2s = [sb([256], F32) for _ in range(4)]
        t3s = [sb([256], F32) for _ in range(4)]
        yvs = [sb([256], F32) for _ in range(4)]
        sq = [{G: sb([4 * 256], BF16) for G in 'hp'} for _ in range(2)]
        yg = [{G: sb([4 * 256], BF16) for G in 'hp'} for _ in range(2)]
        rsg = [sb([4], F32) for _ in range(2)]
        o_sbs = [sb([DM], F32) for _ in range(4)]
        junk2 = sb([DM], BF16)
        sso = [sb([2], F32) for _ in range(2)]
        for kc in range(8):
            S.dma(lambda e, kc=kc: e.dma_start(out=wst2, in_=T['w_out'].ap()[kc * 128:(kc + 1) * 128, :]), writes=['wst2'])
            S.op('dve', lambda e, kc=kc: e.tensor_copy(out=woutb[:, kc * DM:(kc + 1) * DM], in_=wst2), reads=['wst2'], writes=['woutb'])
        S.dma(lambda e: e.dma_start(out=pgbc, in_=AP(T['post_norm_g'], 0, [[0, 128], [1, DM]])), writes=['pgbc'])
        for k in range(4):
            S.dma(lambda e, k=k: e.dma_start(out=cst[:, k:k + 1], in_=AP(T['hyena_d'], 512 + 128 * k, [[1, 128], [1, 1]])), writes=['cstl%d' % (3 * k)])
            S.dma(lambda e, k=k: e.dma_start(out=cst[:, 4 + k:5 + k], in_=AP(T['norm_h_g'], 128 * k, [[1, 128], [1, 1]])), writes=['cstl%d' % (3 * k + 1)])
            S.dma(lambda e, k=k: e.dma_start(out=cst[:, 8 + k:9 + k], in_=AP(T['norm_p_g'], 128 * k, [[1, 128], [1, 1]])), writes=['cstl%d' % (3 * k + 2)])
        S.op('dve', lambda e: e.tensor_copy(out=cst, in_=cst), reads=['cstl%d' % q for q in range(12)], writes=['cst'])
        S.op('dve', lambda e: e.memset(ones, 1.0), writes=['ones'])
        S.op('dve', lambda e: e.memset(epsc, EPS), writes=['epsc'])
        wo3 = woutb.rearrange("p (k n) -> p k n", k=8)
        srcs = {'c2': c2_s, 'y1': y1_s, 'x2': x2_s, 'szh': szh_s, 'ypr': ypr_s, 'szp': szp_s}
        ld3 = [{k: v.rearrange("p (k t) -> p k t", k=4) for k, v in l_.items()} for l_ in ld]
        sq3 = [{k: v.rearrange("p (k t) -> p k t", k=4) for k, v in l_.items()} for l_ in sq]
        yg3 = [{k: v.rearrange("p (k t) -> p k t", k=4) for k, v in l_.items()} for l_ in yg]
        xr3 = [x_.rearrange("p (s d) -> p s d", s=2) for x_ in xr]
        import os
        NTC = int(os.environ.get('CNT', NT))

        def c_load(ti):
            p = ti % 2
            n0 = 256 * ti
            for nm in LDN:
                S.dma(lambda e, nm=nm: e.dma_start(out=ld3[p][nm], in_=AP(srcs[nm], n0, [[HL, 128], [128 * HL, 4], [1, 256]])),
                      writes=['ld%d_%s' % (p, nm)])

        def c_load_x(ti):
            p = ti % 2
            n0 = 256 * ti
            S.dma(lambda e: e.dma_start(out=xr3[p], in_=T['xm'].ap()[HALO + n0:HALO + n0 + 256, :].rearrange("(s p) d -> p s d", p=128)),
                  writes=['xr%d' % p])

        def c_front(ti):
            p = ti % 2
            L_ = ld3[p]
            lr = lambda nm: 'ld%d_%s' % (p, nm)
            for k in range(4):
                t1, t2, t3, yv = t1s[k], t2s[k], t3s[k], yvs[k]
                S.op('dve', lambda e, k=k, t1=t1: e.scalar_tensor_tensor(out=t1, in0=L_['y1'][:, k, :], scalar=cst[:, k:k + 1],
                                                                 in1=L_['c2'][:, k, :], op0=ALU.mult, op1=ALU.add),
                     reads=[lr('y1'), lr('c2'), 'cst'], writes=['t1_%d' % k])
                S.op('dve', lambda e, k=k, t1=t1, t2=t2: e.tensor_tensor(out=t2, in0=t1, in1=L_['x2'][:, k, :], op=ALU.mult),
                     reads=['t1_%d' % k, lr('x2')], writes=['t2_%d' % k])
                S.op('dve', lambda e, k=k, t2=t2, yv=yv: e.tensor_tensor(out=yv, in0=t2, in1=L_['szh'][:, k, :], op=ALU.mult),
                     reads=['t2_%d' % k, lr('szh')], writes=['yv_%d' % k])
                S.op('pool', lambda e, k=k, t3=t3: e.tensor_tensor(out=t3, in0=L_['ypr'][:, k, :], in1=L_['szp'][:, k, :], op=ALU.mult),
                     reads=[lr('ypr'), lr('szp')], writes=['t3_%d' % k])
            for k in range(4):
                t3, yv = t3s[k], yvs[k]
                S.op('act', lambda e, k=k, yv=yv: e.activation(out=sq3[p]['h'][:, k, :], in_=yv, func=AF.Square), reads=['yv_%d' % k], writes=['sq%d_h' % p])
                S.op('act', lambda e, k=k, yv=yv: e.activation(out=yg3[p]['h'][:, k, :], in_=yv, func=AF.Copy, scale=cst[:, 4 + k:5 + k]),
                     reads=['yv_%d' % k, 'cst'], writes=['yg%d_h' % p])
                S.op('act', lambda e, k=k, t3=t3: e.activation(out=sq3[p]['p'][:, k, :], in_=t3, func=AF.Square), reads=['t3_%d' % k], writes=['sq%d_p' % p])
                S.op('act', lambda e, k=k, t3=t3: e.activation(out=yg3[p]['p'][:, k, :], in_=t3, func=AF.Copy, scale=cst[:, 8 + k:9 + k]),
                     reads=['t3_%d' % k, 'cst'], writes=['yg%d_p' % p])

        oi = [0]

        def c_back(ti):
            p = ti % 2
            n0 = 256 * ti
            pss, prs = psum()
            for s_ in range(2):
                for gi, G in enumerate('hp'):
                    col = 2 * s_ + gi
                    for k in range(4):
                        S.op('pe', lambda e, s_=s_, G=G, k=k, col=col: e.matmul(
                            pss[:, col:col + 1], lhsT=sq3[p][G][:, k, s_ * 128:(s_ + 1) * 128], rhs=ones, start=(k == 0), stop=(k == 3)),
                            reads=['sq%d_%s' % (p, G), 'ones'], writes=[prs], quiet=not (s_ == 1 and gi == 1 and k == 3))
            S.op('act', lambda e: e.activation(out=rsg[p], in_=pss[:, 0:4], func=AF.Sqrt, scale=1.0 / 512, bias=epsc[:, 0:1]),
                 reads=[prs, 'epsc'], writes=['rsg%d' % p])
            S.op('dve', lambda e: e.reciprocal(out=rsg[p], in_=rsg[p]), reads=['rsg%d' % p], writes=['rsg%d' % p])
            for s_ in range(2):
                osb = o_sbs[oi[0] % 4]
                oreg = 'o_sb%d' % (oi[0] % 4)
                oi[0] += 1
                for n2 in range(2):
                    pp = {}
                    for gi, G in enumerate('hp'):
                        ps, pr = psum()
                        pp[G] = (ps, pr)
                        for k in range(4):
                            S.op('pe', lambda e, ps=ps, s_=s_, G=G, k=k, gi=gi, n2=n2: e.matmul(
                                ps[:, :], lhsT=yg3[p][G][:, k, s_ * 128:(s_ + 1) * 128], rhs=wo3[:, 4 * gi + k, n2 * 512:(n2 + 1) * 512],
                                start=(k == 0), stop=(k == 3)), reads=['yg%d_%s' % (p, G), 'woutb'], writes=[pr], quiet=(k < 3))
                    S.op('act', lambda e, pp=pp, s_=s_, n2=n2, osb=osb: e.activation(
                        out=osb[:, n2 * 512:(n2 + 1) * 512], in_=pp['h'][0][:, :], func=AF.Copy, scale=rsg[p][:, 2 * s_:2 * s_ + 1]),
                        reads=[pp['h'][1], 'rsg%d' % p], writes=[oreg])
                    S.op('dve', lambda e, pp=pp, s_=s_, n2=n2, osb=osb: e.scalar_tensor_tensor(
                        out=osb[:, n2 * 512:(n2 + 1) * 512], in0=pp['p'][0][:, :], scalar=rsg[p][:, 2 * s_ + 1:2 * s_ + 2],
                        in1=osb[:, n2 * 512:(n2 + 1) * 512], op0=ALU.mult, op1=ALU.add), reads=[pp['p'][1], 'rsg%d' % p, oreg], writes=[oreg])
                S.op('act', lambda e, osb=osb, s_=s_: e.activation(out=junk2, in_=osb, func=AF.Square, accum_out=sso[p][:, s_:s_ + 1]),
                     reads=[oreg], writes=['junk2', 'sso%d' % p])
                S.op('act', lambda e, s_=s_: e.activation(out=sso[p][:, s_:s_ + 1], in_=sso[p][:, s_:s_ + 1], func=AF.Sqrt, scale=1.0 / DM,
                                                         bias=epsc[:, 0:1]), reads=['sso%d' % p, 'epsc'], writes=['sso%d' % p])
                S.op('dve', lambda e, s_=s_: e.reciprocal(out=sso[p][:, s_:s_ + 1], in_=sso[p][:, s_:s_ + 1]), reads=['sso%d' % p], writes=['sso%d' % p])
                S.op('dve', lambda e, osb=osb, s_=s_: e.scalar_tensor_tensor(out=osb, in0=osb, scalar=sso[p][:, s_:s_ + 1], in1=pgbc,
                                                                            op0=ALU.mult, op1=ALU.mult),
                     reads=[oreg, 'sso%d' % p, 'pgbc'], writes=[oreg])
                S.op('pool', lambda e, osb=osb, s_=s_: e.tensor_tensor(out=osb, in0=osb, in1=xr3[p][:, s_, :], op=ALU.add),
                     reads=[oreg, 'xr%d' % p], writes=[oreg])
                S.dma(lambda e, osb=osb, s_=s_: e.dma_start(out=out.ap()[n0 + 128 * s_:n0 + 128 * s_ + 128, :], in_=osb),
                      reads=[oreg], writes=['out%d' % (oi[0] % 4)])

        c_load(0)
        c_load_x(0)
        if NTC > 1:
            c_load(1)
            c_load_x(1)
        c_front(0)
        for ti in range(NTC):
            if ti + 1 < NTC:
                c_front(ti + 1)
            if ti + 2 < NTC:
                c_load(ti + 2)
            c_back(ti)
            if ti + 2 < NTC:
                c_load_x(ti + 2)
    S.emit()
    return nc


_NC_CACHE = {}


def make_in_maps(inputs):
    x = np.asarray(inputs['x'], np.float32)
    sh = shared_tables()
    w = {}
    for nm, shp in WEIGHT_SPECS:
        w[nm] = np.ascontiguousarray(np.asarray(inputs[nm], np.float32).reshape(shp))
    maps = []
    for core in range(8):
        b, h = core // 2, core % 2
        ct = core_tables(h)
        m = dict(w)
        m['xm'] = seg_ext(x[b], HL * h)
        m['xo'] = seg_ext(x[b], HL * (1 - h))
        for nm in ('gtab', 'ctab', 'ztab', 'dtab', 'ident'):
            m[nm] = sh[nm]
        for nm in ('f1tab', 'htab', 'invc'):
            m[nm] = ct[nm]
        maps.append(m)
    return maps


def kernel(**inputs):
    if 'nc' not in _NC_CACHE:
        _NC_CACHE['nc'] = build()
    nc = _NC_CACHE['nc']
    maps = make_in_maps(inputs)
    res = run_bass_kernel_spmd(nc, maps, core_ids=list(range(8)))
    outp = np.zeros((4, L, DM), np.float32)
    for core in range(8):
        b, h = core // 2, core % 2
        outp[b, HL * h:HL * (h + 1)] = np.asarray(res.results[core]['out'], np.float32)
    return outp
```
